# Optimizing a Trainium2 kernel written in Bass

```python
import math
import jax, jax.numpy as jnp
from jax import lax
import numpy as np

D_MODEL = 1024
BATCH = 8
SEQ = 4096
DEPTH = 1

MIX_WIDTH = D_MODEL
GLA_HEADS = 4
GLA_DV = MIX_WIDTH // 2 // GLA_HEADS
GLA_DK = GLA_DV // 2
GLA_GATE_RANK = 16
GLA_GATE_NORM = 16.0
GLA_CHUNK = 64
SWA_HEAD_DIM = 64
SWA_Q_HEADS = MIX_WIDTH // 2 // SWA_HEAD_DIM
SWA_KV_HEADS = 2
SWA_BLOCK = 128
SWA_WINDOW = 128
REL_BUCKETS = 32
REL_MAX_DIST = 128
D_FF = 4 * D_MODEL
NORM_EPS = 1e-6

COL_SIZES = (
    GLA_HEADS * GLA_DK,
    GLA_HEADS * GLA_DK,
    GLA_HEADS * GLA_DV,
    GLA_HEADS * GLA_DV,
    2 * GLA_GATE_RANK,
    SWA_Q_HEADS * SWA_HEAD_DIM,
    SWA_KV_HEADS * SWA_HEAD_DIM,
    SWA_KV_HEADS * SWA_HEAD_DIM,
)
IN_COLS = int(sum(COL_SIZES))
SPLITS = [int(s) for s in np.cumsum(COL_SIZES)[:-1]]

kernel_name = "hybrid_gla_swa_bidir_encoder_layer"


def rmsnorm(x, g):
    xf = x.astype(jnp.float32)
    r = lax.rsqrt(jnp.mean(xf * xf, axis=-1, keepdims=True) + NORM_EPS)
    return (xf * r).astype(x.dtype) * g


def t5_buckets(rel):
    nb = REL_BUCKETS // 2
    ret = (rel > 0).astype(np.int32) * nb
    n = np.abs(rel)
    max_exact = nb // 2
    large = max_exact + (np.log(np.maximum(n, 1).astype(np.float32) / max_exact)
                         / math.log(REL_MAX_DIST / max_exact) * (nb - max_exact)).astype(np.int32)
    large = np.minimum(large, nb - 1)
    return ret + np.where(n < max_exact, n, large)


def gla_chunked(q, k, v, log_a):
    B, H, L, dk = q.shape
    dv = v.shape[-1]
    C = GLA_CHUNK
    N = L // C
    q = q.reshape(B, H, N, C, dk)
    k = k.reshape(B, H, N, C, dk)
    v = v.reshape(B, H, N, C, dv)
    b = jnp.cumsum(log_a.reshape(B, H, N, C, dk), axis=3)
    b_last = b[:, :, :, -1:, :]
    q_dec = q * jnp.exp(b)
    k_intra = k * jnp.exp(-b)
    k_state = k * jnp.exp(b_last - b)
    causal = jnp.asarray(np.tril(np.ones((C, C), dtype=bool)))
    A = jnp.where(causal, jnp.einsum('bhncd,bhnsd->bhncs', q_dec, k_intra), 0.0)
    o_intra = jnp.einsum('bhncs,bhnse->bhnce', A, v)
    dS = jnp.einsum('bhncd,bhnce->bhnde', k_state, v)
    decay = jnp.exp(b_last[:, :, :, 0, :])

    def step(S, inp):
        d, ds = inp
        return d[..., None] * S + ds, S

    S0 = jnp.zeros((B, H, dk, dv), jnp.float32)
    _, S_enter = lax.scan(step, S0, (jnp.moveaxis(decay, 2, 0), jnp.moveaxis(dS, 2, 0)))
    S_enter = jnp.moveaxis(S_enter, 0, 2)
    o_inter = jnp.einsum('bhncd,bhnde->bhnce', q_dec, S_enter)
    return (o_intra + o_inter).reshape(B, H, L, dv)


def banded_window_gqa(q, k, v, sink, rel_table):
    B, Hq, L, dh = q.shape
    Hkv = k.shape[1]
    G = Hq // Hkv
    W = SWA_BLOCK
    N = L // W
    qb = q.reshape(B, Hkv, G, N, W, dh)

    def band(t):
        tp = jnp.pad(t, ((0, 0), (0, 0), (W, W), (0, 0))).reshape(B, Hkv, N + 2, W, dh)
        return jnp.concatenate([tp[:, :, :N], tp[:, :, 1:N + 1], tp[:, :, 2:N + 2]], axis=3)

    kb, vb = band(k), band(v)
    c = np.arange(W)[:, None]
    s = np.arange(3 * W)[None, :]
    rel = s - W - c
    key_pos = (np.arange(N)[:, None, None] - 1) * W + s[None]
    mask = jnp.asarray((np.abs(rel)[None] <= SWA_WINDOW) & (key_pos >= 0) & (key_pos < L))
    bias = rel_table.astype(jnp.float32)[jnp.asarray(t5_buckets(rel))]
    bias = jnp.transpose(bias, (2, 0, 1)).reshape(Hkv, G, 1, W, 3 * W)

    logits = jnp.einsum('bkgncd,bknsd->bkgncs', qb, kb).astype(jnp.float32) * (dh ** -0.5) + bias
    logits = jnp.where(mask, logits, -1e30)
    sink_l = sink.astype(jnp.float32).reshape(Hkv, G, 1, 1, 1)
    m = jnp.maximum(jnp.max(logits, axis=-1, keepdims=True), sink_l)
    p = jnp.exp(logits - m)
    denom = jnp.sum(p, axis=-1, keepdims=True) + jnp.exp(sink_l - m)
    o = jnp.einsum('bkgncs,bknsd->bkgncd', (p / denom).astype(v.dtype), vb)
    return o.reshape(B, Hq, L, dh)


def hybrid_mixer(u, w_in, w_gu_f, b_g_f, w_gu_b, b_g_b, gla_norm, sink, rel_table, w_out):
    B, L, _ = u.shape
    proj = u @ w_in
    qa, ka, va, ga, za, qs, ks, vs = jnp.split(proj, SPLITS, axis=-1)

    def heads(t, h):
        return t.reshape(B, L, h, -1).transpose(0, 2, 1, 3)

    f32 = jnp.float32
    qh = heads(qa, GLA_HEADS).astype(f32) * (GLA_DK ** -0.5)
    kh = heads(ka, GLA_HEADS).astype(f32)
    vh = heads(va, GLA_HEADS).astype(f32)
    zf, zb = za[..., :GLA_GATE_RANK], za[..., GLA_GATE_RANK:]
    la_f = heads(jax.nn.log_sigmoid((zf @ w_gu_f + b_g_f).astype(f32)) / GLA_GATE_NORM, GLA_HEADS)
    la_b = heads(jax.nn.log_sigmoid((zb @ w_gu_b + b_g_b).astype(f32)) / GLA_GATE_NORM, GLA_HEADS)
    o_f = gla_chunked(qh, kh, vh, la_f)
    flip = lambda t: jnp.flip(t, axis=2)
    o_b = flip(gla_chunked(flip(qh), flip(kh), flip(vh), flip(la_b)))
    o_a = o_f + o_b
    o_a = o_a * lax.rsqrt(jnp.mean(o_a * o_a, axis=-1, keepdims=True) + NORM_EPS)
    o_a = o_a.transpose(0, 2, 1, 3) * gla_norm.astype(f32)
    o_a = (o_a.reshape(B, L, GLA_HEADS * GLA_DV) * jax.nn.silu(ga.astype(f32))).astype(u.dtype)

    o_s = banded_window_gqa(heads(qs, SWA_Q_HEADS), heads(ks, SWA_KV_HEADS), heads(vs, SWA_KV_HEADS),
                            sink, rel_table)
    o_s = o_s.transpose(0, 2, 1, 3).reshape(B, L, SWA_Q_HEADS * SWA_HEAD_DIM)

    return jnp.concatenate([o_a, o_s], axis=-1) @ w_out


def setup_inputs(seed: int = 0) -> dict:
    key = jax.random.key(seed)
    ks = jax.random.split(key, 20)
    nrm = lambda k, shape, scale: jax.random.normal(k, shape, jnp.float32) * scale
    gain = lambda k, shape: 1.0 + nrm(k, shape, 0.02)
    gla_w = GLA_HEADS * GLA_DK
    return {
        "x": nrm(ks[0], (BATCH, SEQ, D_MODEL), 1.0),
        "norm_mix_pre": gain(ks[1], (DEPTH, D_MODEL)),
        "w_in": nrm(ks[2], (DEPTH, D_MODEL, IN_COLS), D_MODEL ** -0.5),
        "w_gate_up_fwd": nrm(ks[3], (DEPTH, GLA_GATE_RANK, gla_w), GLA_GATE_RANK ** -0.5),
        "b_gate_fwd": nrm(ks[4], (DEPTH, gla_w), 0.1),
        "w_gate_up_bwd": nrm(ks[5], (DEPTH, GLA_GATE_RANK, gla_w), GLA_GATE_RANK ** -0.5),
        "b_gate_bwd": nrm(ks[6], (DEPTH, gla_w), 0.1),
        "gla_norm": gain(ks[7], (DEPTH, GLA_DV)),
        "swa_sink": nrm(ks[8], (DEPTH, SWA_Q_HEADS), 0.5),
        "rel_bias": nrm(ks[9], (REL_BUCKETS, SWA_Q_HEADS), 0.5),
        "w_out": nrm(ks[10], (DEPTH, MIX_WIDTH, D_MODEL), MIX_WIDTH ** -0.5),
        "norm_mix_post": gain(ks[11], (DEPTH, D_MODEL)),
        "norm_mlp_pre": gain(ks[12], (DEPTH, D_MODEL)),
        "w_up": nrm(ks[13], (DEPTH, D_MODEL, D_FF), D_MODEL ** -0.5),
        "w_down": nrm(ks[14], (DEPTH, D_FF, D_MODEL), D_FF ** -0.5),
        "norm_mlp_post": gain(ks[15], (DEPTH, D_MODEL)),
    }


def reference(x, norm_mix_pre, w_in, w_gate_up_fwd, b_gate_fwd, w_gate_up_bwd, b_gate_bwd,
              gla_norm, swa_sink, rel_bias, w_out, norm_mix_post, norm_mlp_pre, w_up, w_down,
              norm_mlp_post):
    h = x
    for l in range(DEPTH):
        u = rmsnorm(h, norm_mix_pre[l])
        mix = hybrid_mixer(u, w_in[l], w_gate_up_fwd[l], b_gate_fwd[l], w_gate_up_bwd[l], b_gate_bwd[l],
                           gla_norm[l], swa_sink[l], rel_bias, w_out[l])
        h = h + rmsnorm(mix, norm_mix_post[l])
        z = rmsnorm(h, norm_mlp_pre[l]) @ w_up[l]
        ff = jnp.square(jax.nn.relu(z)) @ w_down[l]
        h = h + rmsnorm(ff, norm_mlp_post[l])
    return h
```

```python
import math
from contextlib import ExitStack

import numpy as np
import concourse.bass as bass
import concourse.mybir as mybir
from concourse.bass_utils import run_bass_kernel_spmd
from concourse.alu_op_type import AluOpType as ALU

F32 = mybir.dt.float32
BF16 = mybir.dt.bfloat16
AF = mybir.ActivationFunctionType
AX = mybir.AxisListType

D = 1024
DFF = 4096
NCOL = 2816
EPS = 1e-6
NEG = -30000.0
ENGS = ("pe", "act", "dve", "pool", "sp")


class _Op:
    __slots__ = ("eng", "fn", "deps", "dma", "sig", "sem", "val", "prev", "idx")


def _norm(r):
    if isinstance(r, str):
        return (r, None)
    return (r[0], r[1] if len(r) == 2 else tuple(r[1:]))


class Prog:
    def __init__(self):
        self.ops = []
        self.st = {}
        self.last = {}
        self.bar_deps = set()
        self.bar_pending = set()
        self.dma_since_bar = []

    def _ents(self, name, key):
        d = self.st.setdefault(name, {})
        if key is None:
            return list(d.values())
        return [d[k] for k in (None, key) if k in d]

    def add(self, eng, fn, reads=(), writes=(), dma=False):
        op = _Op()
        op.eng, op.fn, op.dma, op.sig, op.sem, op.val = eng, fn, dma, False, None, 0
        op.idx = len(self.ops)
        deps = set()
        reads = [_norm(r) for r in reads]
        writes = [_norm(w) for w in writes]
        for name, key in reads:
            for e in self._ents(name, key):
                if e[0] is not None:
                    deps.add(e[0])
        for name, key in writes:
            for e in self._ents(name, key):
                if e[0] is not None:
                    deps.add(e[0])
                deps.update(e[1])
        if eng in self.bar_pending:
            deps |= self.bar_deps
            self.bar_pending.discard(eng)
        deps.discard(op)
        op.deps = {d for d in deps if not (d.eng == "pe" and eng == "pe" and not d.dma)}
        for name, key in reads:
            d = self.st[name]
            ent = d.get(key)
            if ent is None:
                ent = d[key] = [None, []]
            if not dma:
                ent[1] = [r for r in ent[1] if r.dma or r.eng != eng]
            ent[1].append(op)
        for name, key in writes:
            d = self.st[name]
            if key is None:
                d.clear()
            d[key] = [op, []]
        self.ops.append(op)
        if dma:
            self.dma_since_bar.append(op)
        else:
            self.last[eng] = op
        return op

    def barrier(self):
        self.bar_deps = set(self.last.values()) | set(self.dma_since_bar)
        self.dma_since_bar = []
        self.bar_pending = set(ENGS)

    def emit(self, nc, es, ndma=8):
        handles = {"pe": nc.tensor, "act": nc.scalar, "dve": nc.vector,
                   "pool": nc.gpsimd, "sp": nc.sync}
        for op in self.ops:
            if op.dma:
                op.sig = True
            for d in op.deps:
                d.sig = True
        csem = {e: es.enter_context(nc.semaphore("c_" + e)) for e in ENGS}
        dsem = {e: [es.enter_context(nc.semaphore("d_%s%d" % (e, i))) for i in range(ndma)]
                for e in ("sp", "pool", "act")}
        ccnt = {e: 0 for e in ENGS}
        drr = {e: 0 for e in dsem}
        dcnt = {}
        for op in self.ops:
            if not op.sig:
                continue
            if op.dma:
                k = drr[op.eng] % ndma
                drr[op.eng] += 1
                dcnt[(op.eng, k)] = dcnt.get((op.eng, k), 0) + 16
                op.sem, op.val = dsem[op.eng][k], dcnt[(op.eng, k)]
                op.prev = op.val - 16
            else:
                ccnt[op.eng] += 1
                op.sem, op.val = csem[op.eng], ccnt[op.eng]
        per = {e: [o for o in self.ops if o.eng == e] for e in ENGS}
        nwaits = {e: 0 for e in ENGS}

        def run(eng, h):
            waited = {}
            for op in per[eng]:
                need = {}
                for d in op.deps:
                    k = id(d.sem)
                    if k not in need or need[k][1] < d.val:
                        need[k] = (d.sem, d.val)
                if op.dma and op.sig and op.prev > 0:
                    k = id(op.sem)
                    if k not in need or need[k][1] < op.prev:
                        need[k] = (op.sem, op.prev)
                for k, (sem, val) in need.items():
                    if waited.get(k, 0) < val:
                        h.wait_ge(sem, val)
                        waited[k] = val
                        nwaits[eng] += 1
                if op.fn is not None:
                    ins = op.fn(h)
                    if op.sig:
                        ins.then_inc(op.sem, 16 if op.dma else 1)

        block = es.enter_context(nc.Block())

        @block.tensor
        def _(e):
            run("pe", e)

        @block.scalar
        def _(e):
            run("act", e)

        @block.vector
        def _(e):
            run("dve", e)

        @block.gpsimd
        def _(e):
            run("pool", e)

        @block.sync
        def _(e):
            run("sp", e)

        self.stats = {e: (len(per[e]), nwaits[e]) for e in ENGS}


def build_program(NT, debug=False, arena_kib=207, upto=3):
    L = NT * 128
    nc = bass.Bass("TRN2", target_bir_lowering=False)
    es = ExitStack()
    P = Prog()

    def din(name, shape):
        return nc.dram_tensor(name, list(shape), F32, kind="ExternalInput").ap()

    x_d = din("x", [L, D])
    g_pre_d = din("norm_mix_pre", [1, D])
    w_in_d = din("w_in", [D, 2336])
    wgf_d = din("w_gate_up_fwd", [16, 256])
    bgf_d = din("b_gate_fwd", [1, 256])
    wgb_d = din("w_gate_up_bwd", [16, 256])
    bgb_d = din("b_gate_bwd", [1, 256])
    gn_d = din("gla_norm", [1, 128])
    sink_d = din("swa_sink", [1, 8])
    relb_d = din("rel_bias", [32, 8])
    w_out_d = din("w_out", [D, D])
    g_post_d = din("norm_mix_post", [1, D])
    g_mpre_d = din("norm_mlp_pre", [1, D])
    w_up_d = din("w_up", [D, DFF])
    w_down_d = din("w_down", [DFF, D])
    g_mpost_d = din("norm_mlp_post", [1, D])
    c_tri_d = din("c_tri", [4, 128, 128])
    c_mask_d = din("c_mask", [2, 128, 128])
    c_ident_d = din("c_ident", [128, 128])
    c_flip_d = din("c_flip", [128, 128])
    c_oh_d = din("c_oh", [33, 512])
    out_d = nc.dram_tensor("out", [L, D], F32, kind="ExternalOutput").ap()
    hbuf_d = nc.dram_tensor("hbuf", [L, D], F32,
                            kind="ExternalOutput" if debug else "Internal").ap()
    vd_t = nc.dram_tensor("vd_scr", [8, 512], F32)
    vd_d = vd_t.ap()

    AW = arena_kib * 256
    arena = es.enter_context(nc.sbuf_tensor("arena", [128, AW], F32))
    state = {"off": 0, "peak": 0}

    def alloc(shape, dt=F32, parts=128):
        n = 1
        for s in shape:
            n *= s
        words = n if dt == F32 else (n + 1) // 2
        words = (words + 15) // 16 * 16
        off = state["off"]
        assert off + words <= AW, ("SBUF arena overflow", off, words, AW)
        state["off"] = off + words
        state["peak"] = max(state["peak"], state["off"])
        v = arena[0:parts, off:off + words]
        if dt != F32:
            v = v.bitcast(dt)
        v = v[:, 0:n]
        if len(shape) == 2:
            v = v.rearrange("p (a b) -> p a b", a=shape[0])
        elif len(shape) == 3:
            v = v.rearrange("p (a b c) -> p a b c", a=shape[0], b=shape[1])
        return v

    banks = [es.enter_context(nc.psum_tensor("psb%d" % i, [128, 512], F32)) for i in range(8)]
    bank_f = [b[:, :] for b in banks]
    bank_h = [b[:, :].bitcast(BF16) for b in banks]
    classes = {"T": [0, 1], "P": [2, 3, 4], "G": [5, 6, 7]}
    rr = {k: 0 for k in classes}

    touch = {}

    def pbank(cls):
        best, bt = None, None
        for i in range(8):
            nm = "ps%d" % i
            tch = touch.get(nm, -1)
            ents = P.st.get(nm, {})
            for e in ents.values():
                if e[0] is not None:
                    tch = max(tch, e[0].idx)
                for r in e[1]:
                    tch = max(tch, r.idx)
            if bt is None or tch < bt:
                best, bt = i, tch
        touch["ps%d" % best] = len(P.ops)
        return best, "ps%d" % best

    def dma(eng, out, in_, reads, writes):
        return P.add(eng, lambda e: e.dma_start(out=out, in_=in_), reads, writes, dma=True)

    def act(out, in_, func, reads, writes, **kw):
        return P.add("act", lambda e: e.activation(out=out, in_=in_, func=func, **kw), reads, writes)

    def mm(out, lhsT, rhs, start, stop, reads, writes):
        return P.add("pe", lambda e: e.matmul(out, lhsT, rhs, start=start, stop=stop), reads, writes)

    def tt(eng, out, in0, in1, op, reads, writes):
        return P.add(eng, lambda e: e.tensor_tensor(out=out, in0=in0, in1=in1, op=op), reads, writes)

    def stt(out, in0, scalar, in1, op0, op1, reads, writes):
        return P.add("dve", lambda e: e.scalar_tensor_tensor(out=out, in0=in0, scalar=scalar, in1=in1,
                                                             op0=op0, op1=op1), reads, writes)

    def ts(eng, out, in0, s1, s2, op0, op1, reads, writes):
        return P.add(eng, lambda e: e.tensor_scalar(out=out, in0=in0, scalar1=s1, scalar2=s2,
                                                    op0=op0, op1=op1), reads, writes)

    def cp(eng, out, in_, reads, writes):
        if eng == "act":
            return act(out, in_, AF.Copy, reads, writes)
        return P.add(eng, lambda e: e.tensor_copy(out=out, in_=in_), reads, writes)

    def mset(eng, out, val, writes):
        return P.add(eng, lambda e: e.memset(out, val), (), writes)

    def bcast(ap_row):
        return ap_row.partition_broadcast(128)

    ident_b = alloc([128], BF16)
    gmpre = alloc([D]); gmpost = alloc([D])
    eps_t = alloc([1])
    mark_persist = state["off"]
    ident_f = alloc([128]); flipJ = alloc([128]); tri = alloc([4, 128])
    maskF4 = alloc([512], BF16); maskB4 = alloc([512], BF16)
    gpre = alloc([D]); gpost = alloc([D])
    gncol = alloc([1])
    esink = alloc([8])
    ones2 = alloc([128], BF16, parts=33)
    Rb = alloc([512], BF16, parts=33)

    dma("sp", ident_f, c_ident_d, (), ["ident_f"])
    dma("sp", flipJ, c_flip_d, (), ["flipJ"])
    dma("sp", tri, c_tri_d.rearrange("k p c -> p k c"), (), ["tri"])
    dma("sp", gpre, bcast(g_pre_d), (), ["gpre"])
    dma("sp", gpost, bcast(g_post_d), (), ["gpost"])
    dma("sp", gmpre, bcast(g_mpre_d), (), ["gmpre"])
    dma("sp", gmpost, bcast(g_mpost_d), (), ["gmpost"])
    dma("sp", gncol, gn_d.rearrange("o e -> e o"), (), ["gncol"])
    dma("sp", esink, bcast(sink_d), (), ["esink"])
    act(esink, esink, AF.Exp, ["esink"], ["esink"])
    cp("dve", ident_b, ident_f, ["ident_f"], ["ident_b"])
    mset("dve", eps_t, EPS, ["eps_t"])

    W_all = alloc([8, NCOL], BF16)
    W_out = alloc([8, D], BF16)
    Sb_all = alloc([NT, 2, 128], BF16)
    biasT = alloc([3, 8, 128])
    mark_stream = state["off"]

    mtmp = alloc([2, 128])
    wz32 = alloc([8, 32])
    wzT = alloc([1024], parts=32)
    BD = alloc([512], parts=32)
    b32 = alloc([512], parts=33)
    hi_b = alloc([512], BF16, parts=33)
    lo_f = alloc([512], parts=33)
    tabext = alloc([8], parts=33)
    OH = alloc([512], parts=33)
    Vsb = alloc([512], parts=8)
    Hk = alloc([3, 8, 128])
    wo32 = alloc([4, D])

    dma("sp", mtmp, c_mask_d.rearrange("k p c -> p k c"), (), ["mtmp"])
    for h in range(4):
        cp("dve", maskF4[:, h * 128:(h + 1) * 128], mtmp[:, 0, :], ["mtmp"], [("maskF4", h)])
        cp("dve", maskB4[:, h * 128:(h + 1) * 128], mtmp[:, 1, :], ["mtmp"], [("maskB4", h)])

    w_in_v = w_in_d.rearrange("(kc p) c -> p kc c", p=128)
    dma("pool", W_all[:, :, 0:1536], w_in_v[:, :, 0:1536], (), [("W_all", "main")])
    for kc in range(8):
        for kv in range(2):
            src = w_in_d[kc * 128:(kc + 1) * 128, 1568 + kv * 256:1568 + (kv + 1) * 256].rearrange(
                "p (g d) -> p g d", g=4)
            dst = W_all[:, kc, 2048:2560].rearrange("p (g kv d) -> p kv g d", g=4, kv=2)[:, kv, :, :]
            dma("pool", dst, src, (), [("W_all", "qs%d_%d" % (kc, kv))])
    dma("pool", W_all[:, :, 2560:2816], w_in_v[:, :, 2080:2336], (), [("W_all", "kv")])
    w_out_v = w_out_d.rearrange("(kc p) c -> p kc c", p=128)
    dma("pool", W_out[:, 4:8, :], w_out_v[:, 4:8, :], (), [("W_out", "swa")])
    dma("sp", wo32, w_out_v[:, 0:4, :], (), ["wo32"])
    for kc in range(4):
        if kc % 2:
            act(W_out[:, kc, :], wo32[:, kc, :], AF.Copy, ["wo32", "gncol"], [("W_out", kc)],
                scale=gncol[:, 0:1])
        else:
            ts("dve", W_out[:, kc, :], wo32[:, kc, :], gncol[:, 0:1], None, ALU.mult, ALU.bypass,
               ["wo32", "gncol"], [("W_out", kc)])

    dma("sp", wz32, w_in_v[:, :, 1536:1568], (), ["wz32"])
    mset("dve", BD, 0.0, ["BD"])
    dma("sp", BD[0:16, 0:256], wgf_d, (), ["BD"])
    dma("sp", BD[16:32, 256:512], wgb_d, (), ["BD"])
    for half in range(2):
        bi, bn = pbank("P")
        for k4 in range(4):
            kc = half * 4 + k4
            P.add("pe", lambda e, kc=kc, k4=k4, bi=bi: e.transpose(
                bank_f[bi][0:32, k4 * 128:(k4 + 1) * 128], wz32[:, kc, :], ident_f),
                ["wz32", "ident_f"], [bn])
        cp("act", wzT[:, half * 512:(half + 1) * 512], bank_f[bi][0:32, :], [bn], [("wzT", half)])
    for kc in range(8):
        bi, bn = pbank("P")
        mm(bank_f[bi], wzT[:, kc * 128:(kc + 1) * 128], BD, True, True, ["wzT", "BD"], [bn])
        cp("act" if kc % 2 else "dve", W_all[:, kc, 1536:2048], bank_f[bi], [bn], [("W_all", "z%d" % kc)])

    mset("dve", b32, 0.0, ["b32"])
    for r in (0, 32):
        dma("sp", b32[r:r + 1, 0:256], bgf_d, (), ["b32"])
        dma("sp", b32[r:r + 1, 256:512], bgb_d, (), ["b32"])
    cp("dve", hi_b, b32, ["b32"], ["hi_b"])
    tt("dve", lo_f, b32, hi_b, ALU.subtract, ["b32", "hi_b"], ["lo_f"])
    cp("dve", Rb, hi_b, ["hi_b"], ["Rb"])
    cp("dve", Rb[32:33, :], lo_f[32:33, :], ["lo_f"], ["Rb"])
    mset("dve", ones2, 0.0, ["ones2"])
    mset("dve", ones2[0:1, :], 1.0, ["ones2"])
    mset("dve", ones2[32:33, :], 1.0, ["ones2"])

    mset("dve", tabext, NEG, ["tabext"])
    dma("sp", tabext[0:32, :], relb_d, (), ["tabext"])
    dma("sp", OH, c_oh_d, (), ["OH"])
    bi, bn = pbank("G")
    mm(bank_f[bi][0:8, :], tabext, OH, True, True, ["tabext", "OH"], [bn])
    cp("act", Vsb, bank_f[bi][0:8, :], [bn], ["Vsb"])
    dma("sp", vd_d, Vsb, ["Vsb"], ["vd"])
    for jj in range(3):
        j = jj - 1
        src = bass.AP(vd_d.tensor, j * 128 + 129, [[1, 128], [512, 8], [1, 128]])
        dma("sp", Hk[:, jj, :, :], src, ["vd"], [("Hk", jj)])
    for jj in range(3):
        for hh in range(2):
            bi, bn = pbank("G")
            for h4 in range(4):
                h = hh * 4 + h4
                mm(bank_f[bi][:, h4 * 128:(h4 + 1) * 128], Hk[:, jj, h, :], flipJ, True, True,
                   [("Hk", jj), "flipJ"], [bn])
            cp("act" if hh else "dve", biasT[:, jj, hh * 4:(hh + 1) * 4, :],
               bank_f[bi].rearrange("p (a b) -> p a b", a=4), [bn], [("biasT", jj, hh)])

    P.barrier()
    state["off"] = mark_stream

    NXS = 2
    xs = [alloc([D]) for _ in range(NXS)]
    junk = alloc([D], BF16)
    u_bf = alloc([D], BF16)
    uT = [alloc([8, 128], BF16) for _ in range(2)]
    st4 = [alloc([4]) for _ in range(4)]
    qk_bf = [alloc([512], BF16) for _ in range(2)]
    v_bf = [alloc([512], BF16) for _ in range(2)]
    sp_t = [alloc([512]) for _ in range(2)]
    t2 = [alloc([512]) for _ in range(2)]
    ks_bf = alloc([128], BF16)
    qs_bf = alloc([512], BF16)
    ksT = [alloc([2, 128], BF16) for _ in range(4)]
    vsa = [alloc([2, 65], BF16) for _ in range(5)]
    qsT = [alloc([4, 128], BF16) for _ in range(3)]
    eP = alloc([512]); eN = alloc([512]); ekf = alloc([256])
    kst = [alloc([256], BF16) for _ in range(2)]
    qd = [alloc([4, 128], BF16) for _ in range(2)]
    ki = [alloc([256], BF16) for _ in range(2)]
    ATf = alloc([512], BF16); ATb = alloc([512], BF16)
    S32 = alloc([2, 128]); S_bf = [alloc([2, 128], BF16) for _ in range(2)]
    osq = alloc([512])
    mixcat = [alloc([D], BF16) for _ in range(3)]
    mixT = alloc([8, 128], BF16)
    lgs = [alloc([512]) for _ in range(2)]
    pT = [[[alloc([512], BF16) for _ in range(3)] for _ in range(2)] for _ in range(2)]
    den = alloc([8])
    hs = [alloc([D]) for _ in range(2)]
    xr = [alloc([D]) for _ in range(2)]
    Sb32 = alloc([2, 128])
    dec = [alloc([4]) for _ in range(2)]

    for i in range(5):
        mset("pool", vsa[i], 1.0, [("vsa", i)])
    for i in range(4):
        mset("pool", ksT[i], 0.0, [("ksT", i, 0), ("ksT", i, 1)])
    for d in range(2):
        mset("pool", qd[d], 0.0, [("qd", d, 0), ("qd", d, 1)])
    mset("pool", S32, 0.0, [("S32", 0), ("S32", 1)])
    mset("pool", S_bf[0], 0.0, [("S_bf", 0)])
    mset("pool", Sb32, 0.0, [("Sb32", 0), ("Sb32", 1)])
    mset("pool", Sb_all[:, NT - 1, :, :], 0.0, [("Sb_all", NT - 1)])

    def load_x(n):
        slot = n % NXS
        dma("sp", xs[slot], x_d[n * 128:(n + 1) * 128, :], (), [("xs", slot)])

    def rstd_from_ss(ss_ap, out_ap, inv_n, rd, wr):
        act(out_ap, ss_ap, AF.Ln, rd, wr, scale=inv_n, bias=eps_t[:, 0:1])
        act(out_ap, out_ap, AF.Exp, wr, wr, scale=-0.5)

    def normT_stats(n):
        slot = n % NXS
        act(junk, xs[slot], AF.Square, [("xs", slot)], ["junk", "st0"], accum_out=st4[0][:, 0:1])
        rstd_from_ss(st4[0][:, 0:1], st4[0][:, 1:2], 1.0 / D, ["st0", "eps_t"], ["st0"])
        stt(u_bf, xs[slot], st4[0][:, 1:2], gpre, ALU.mult, ALU.mult, [("xs", slot), "st0", "gpre"], ["u_bf"])

    def normT_tr(n):
        us = n % 2
        bi, bn = pbank("T")
        for kc in range(8):
            P.add("pe", lambda e, kc=kc, bi=bi: e.transpose(
                bank_h[bi][:, kc * 128:(kc + 1) * 128], u_bf[:, kc * 128:(kc + 1) * 128], ident_b),
                ["u_bf", "ident_b"], [bn])
        cp("act", uT[us], bank_h[bi].rearrange("p (a b) -> p a b", a=8), [bn], [("uT", us)])

    def proj(uslot, lo, hi, bias_cols=None):
        bi, bn = pbank("P")
        n = hi - lo
        for kc in range(8):
            mm(bank_f[bi][:, 0:n], uT[uslot][:, kc, :], W_all[:, kc, lo:hi], kc == 0,
               kc == 7 and bias_cols is None, [("uT", uslot), "W_all"], [bn])
        if bias_cols is not None:
            mm(bank_f[bi][:, 0:n], ones2, Rb[:, bias_cols[0]:bias_cols[1]], False, True,
               ["ones2", "Rb"], [bn])
        return bi, bn

    def softplus_neg(dst, dst_reg, src, src_reg):
        act(dst, src, AF.Exp, [src_reg], [dst_reg], scale=-1.0)
        act(dst, dst, AF.Ln, [dst_reg], [dst_reg], bias=1.0)

    def pre_front(n):
        us = n % 2
        sl = n % 2
        bL, nL = proj(us, 1792, 2048, bias_cols=(256, 512))
        spb = sp_t[sl][:, 0:256]
        softplus_neg(spb, ("sp", sl), bank_f[bL][:, 0:256], nL)
        bK, nK = proj(us, 256, 512)
        bV, nV = proj(us, 512, 1024)
        cp("act", v_bf[sl], bank_f[bV], [nV], [("v_bf", sl)])
        return (bK, nK, spb, sl)

    def pre_mid(n, ctx):
        bK, nK, spb, sl = ctx
        bG, nG = pbank("G")
        mm(bank_f[bG][:, 0:256], tri[:, 2, :], spb, True, True, ["tri", ("sp", sl)], [nG])
        for hp in range(2):
            mm(bank_f[bG][:, 256 + 2 * hp:258 + 2 * hp], spb[:, hp * 128:(hp + 1) * 128],
               tri[:, 1, 0:2], True, True, ["tri", ("sp", sl)], [nG])
        act(ekf, bank_f[bG][:, 0:256], AF.Exp, [nG], ["ekf"])
        act(dec[sl], bank_f[bG][:, 256:260], AF.Exp, [nG], [("dec", sl)])
        tt("dve", kst[sl], bank_f[bK][:, 0:256], ekf, ALU.mult, [nK, "ekf"], [("kst", sl)])

    def pre_back(n):
        sl = n % 2
        bS, nS = pbank("G")
        for h in range(4):
            mm(bank_f[bS][(h % 2) * 64:(h % 2) * 64 + 64, (h // 2) * 128:(h // 2 + 1) * 128],
               kst[sl][:, h * 64:(h + 1) * 64], v_bf[sl][:, h * 128:(h + 1) * 128], True, True,
               [("kst", sl), ("v_bf", sl)], [nS])
        for hp in range(2):
            stt(Sb32[:, hp, :], Sb32[:, hp, :], dec[sl][:, 2 * hp:2 * hp + 1],
                bank_f[bS][:, hp * 128:(hp + 1) * 128], ALU.mult, ALU.add,
                [("Sb32", hp), ("dec", sl), nS], [("Sb32", hp)])
        cp("act", Sb_all[:, n - 1, :, :], Sb32, [("Sb32", 0), ("Sb32", 1)], [("Sb_all", n - 1)])

    if upto >= 1 and NT > 1:
        load_x(NT - 1)
        if NT - 2 >= 0:
            load_x(NT - 2)
        normT_stats(NT - 1)
        normT_tr(NT - 1)
        for n in range(NT - 1, 0, -1):
            nxt = n - 1 if n - 1 >= 1 else (0 if upto >= 2 else None)
            if nxt is not None:
                normT_stats(nxt)
                nn = nxt - 1 if nxt >= 1 else 1
                if 0 <= nn < NT and not (nxt == 0 and NT < 2):
                    load_x(nn)
            ctx = pre_front(n)
            if nxt is not None:
                normT_tr(nxt)
            if n + 1 <= NT - 1:
                pre_back(n + 1)
            pre_mid(n, ctx)
        pre_back(1)
    elif upto >= 2:
        load_x(0)
        if NT > 1:
            load_x(1)
        normT_stats(0)
        normT_tr(0)

    def swa_logits(t, kv, only_j=None):
        js = [j for j in (-1, 0, 1) if 0 <= t + j < NT and (only_j is None or j == only_j)]
        for j in js:
            bL, nL = pbank("G")
            k4 = (t + j) % 4
            mm(bank_f[bL], ksT[k4][:, kv, :], qsT[t % 3].rearrange("p a b -> p (a b)"),
               True, True, [("ksT", k4, 0), ("ksT", k4, 1), ("qsT", t % 3)], [nL])
            li = (kv * 3 + j + 1) % 2
            stt(lgs[li], bank_f[bL], 0.125,
                biasT[:, j + 1, kv * 4:(kv + 1) * 4, :].rearrange("p a b -> p (a b)"),
                ALU.mult, ALU.add, [nL, "biasT"], [("lgs", li)])
            act(pT[t % 2][kv][j + 1], lgs[li], AF.Exp, [("lgs", li)], [("pT", t % 2, kv, j + 1)])

    def projA(n):
        us = n % 2
        sl = n % 2
        bD, nD = proj(us, 1536, 2048, bias_cols=(0, 512))
        softplus_neg(sp_t[sl], ("sp", sl), bank_f[bD], nD)

    def projB(n):
        us = n % 2
        sl = n % 2
        bA, nA = proj(us, 0, 512)
        act(qk_bf[sl][:, 0:256], bank_f[bA][:, 0:256], AF.Copy, [nA], [("qk_bf", sl)], scale=0.125)
        cp("dve", qk_bf[sl][:, 256:512], bank_f[bA][:, 256:512], [nA], [("qk_bf", sl)])

    def projC(n):
        us = n % 2
        sl = n % 2
        bB, nB = proj(us, 512, 1024)
        cp("act", v_bf[sl], bank_f[bB], [nB], [("v_bf", sl)])

    def projG(n):
        us = n % 2
        sl = n % 2
        bC, nC = proj(us, 1024, 1536)
        act(t2[sl], bank_f[bC], AF.Exp, [nC], [("t2", sl)], scale=-1.0)
        act(t2[sl], t2[sl], AF.Ln, [("t2", sl)], [("t2", sl)], bias=1.0)
        act(t2[sl], t2[sl], AF.Exp, [("t2", sl)], [("t2", sl)], scale=-1.0)
        tt("dve", t2[sl], bank_f[bC], t2[sl], ALU.mult, [nC, ("t2", sl)], [("t2", sl)])

    def projC2(n):
        us = n % 2
        bF, nF = proj(us, 2560, 2816)
        cp("act", ks_bf, bank_f[bF][:, 0:128], [nF], ["ks_bf"])
        cp("dve", vsa[n % 5][:, :, 0:64], bank_f[bF][:, 128:256].rearrange("p (a b) -> p a b", a=2),
           [nF], [("vsa", n % 5)])

    def projC2_tr(n):
        k4 = n % 4
        bT, nT_ = pbank("T")
        P.add("pe", lambda e, bT=bT: e.transpose(bank_h[bT][:, 0:128], ks_bf, ident_b),
              ["ks_bf", "ident_b"], [nT_])
        for kv in range(2):
            cp("dve", ksT[k4][kv * 64:(kv + 1) * 64, kv, :], bank_h[bT][kv * 64:(kv + 1) * 64, 0:128],
               [nT_], [("ksT", k4, kv)])

    def projC3(n):
        us = n % 2
        bE, nE = proj(us, 2048, 2560)
        cp("act", qs_bf, bank_f[bE], [nE], ["qs_bf"])

    def projC3_tr(n):
        bT, nT_ = pbank("T")
        for g in range(4):
            P.add("pe", lambda e, g=g, bT=bT: e.transpose(
                bank_h[bT][:, g * 128:(g + 1) * 128], qs_bf[:, g * 128:(g + 1) * 128], ident_b),
                ["qs_bf", "ident_b"], [nT_])
        cp("dve", qsT[n % 3], bank_h[bT][:, 0:512].rearrange("p (a b) -> p a b", a=4),
           [nT_], [("qsT", n % 3)])

    def S2a(m):
        sl = m % 2
        bQ, nQ = pbank("T")
        for blk in range(4):
            P.add("pe", lambda e, blk=blk, bQ=bQ: e.transpose(
                bank_h[bQ][:, blk * 128:(blk + 1) * 128], qk_bf[sl][:, blk * 128:(blk + 1) * 128], ident_b),
                [("qk_bf", sl), "ident_b"], [nQ])
        bB, nB = pbank("G")
        for hp in range(2):
            mm(bank_f[bB][:, hp * 128:(hp + 1) * 128], sp_t[sl][:, hp * 128:(hp + 1) * 128],
               tri[:, 0, :], True, True, [("sp", sl), "tri"], [nB])
        for hp in range(2):
            mm(bank_f[bB][:, 256 + hp * 128:256 + (hp + 1) * 128],
               sp_t[sl][:, 256 + hp * 128:256 + (hp + 1) * 128], tri[:, 1, :], True, True,
               [("sp", sl), "tri"], [nB])
        bE, nE = pbank("G")
        mm(bank_f[bE][:, 0:256], tri[:, 3, :], sp_t[sl][:, 0:256], True, True, [("sp", sl), "tri"], [nE])
        act(eP, bank_f[bB], AF.Exp, [nB], ["eP"])
        act(eN, bank_f[bB], AF.Exp, [nB], ["eN"], scale=-1.0)
        act(ekf, bank_f[bE][:, 0:256], AF.Exp, [nE], ["ekf"])
        for d in range(2):
            for par in range(2):
                r0 = par * 64
                tt("dve",
                   qd[d][r0:r0 + 64, :, :].rearrange("p (hp par) c -> p hp par c", par=2)[:, :, par, :],
                   bank_h[bQ][r0:r0 + 64, 0:256].rearrange("p (hp c) -> p hp c", hp=2),
                   eP[r0:r0 + 64, d * 256:(d + 1) * 256].rearrange("p (hp c) -> p hp c", hp=2),
                   ALU.mult, [nQ, "eP"], [("qd", d, par)])
            tt("dve", ki[d], bank_h[bQ][:, 256:512], eN[:, d * 256:(d + 1) * 256], ALU.mult,
               [nQ, "eN"], [("ki", d)])
        tt("dve", kst[0], qk_bf[sl][:, 256:512], ekf, ALU.mult, [("qk_bf", sl), "ekf"], [("kst", 0)])

    def S2b(m):
        bAf, nAf = pbank("G")
        for h in range(4):
            c0 = (h // 2) * 128
            mm(bank_f[bAf][:, h * 128:(h + 1) * 128], ki[0][:, c0:c0 + 128],
               qd[0][:, h, :], True, True, [("ki", 0), ("qd", 0, 0), ("qd", 0, 1)], [nAf])
        tt("dve", ATf, bank_f[bAf], maskF4, ALU.mult, [nAf, "maskF4"], ["ATf"])
        bAb, nAb = pbank("G")
        for h in range(4):
            c0 = (h // 2) * 128
            mm(bank_f[bAb][:, h * 128:(h + 1) * 128], ki[1][:, c0:c0 + 128],
               qd[1][:, h, :], True, True, [("ki", 1), ("qd", 1, 0), ("qd", 1, 1)], [nAb])
        tt("dve", ATb, bank_f[bAb], maskB4, ALU.mult, [nAb, "maskB4"], ["ATb"])

    def S2c(m):
        sl = m % 2
        bO, nO = pbank("G")
        for h in range(4):
            hp = h // 2
            oo = bank_f[bO][:, h * 128:(h + 1) * 128]
            vv = v_bf[sl][:, h * 128:(h + 1) * 128]
            mm(oo, ATf[:, h * 128:(h + 1) * 128], vv, True, False, ["ATf", ("v_bf", sl)], [nO])
            mm(oo, qd[0][:, h, :], S_bf[sl][:, hp, :], False, False,
               [("qd", 0, 0), ("qd", 0, 1), ("S_bf", sl)], [nO])
            mm(oo, ATb[:, h * 128:(h + 1) * 128], vv, False, False, ["ATb", ("v_bf", sl)], [nO])
            mm(oo, qd[1][:, h, :], Sb_all[:, m, hp, :], False, True,
               [("qd", 1, 0), ("qd", 1, 1), ("Sb_all", m)], [nO])
        bS, nS = pbank("G")
        for h in range(4):
            mm(bank_f[bS][(h % 2) * 64:(h % 2) * 64 + 64, (h // 2) * 128:(h // 2 + 1) * 128],
               kst[0][:, h * 64:(h + 1) * 64], v_bf[sl][:, h * 128:(h + 1) * 128], True, True,
               [("kst", 0), ("v_bf", sl)], [nS])
        for hp in range(2):
            stt(S32[:, hp, :], S32[:, hp, :], eP[:, hp * 128 + 127:hp * 128 + 128],
                bank_f[bS][:, hp * 128:(hp + 1) * 128], ALU.mult, ALU.add, [("S32", hp), "eP", nS], [("S32", hp)])
        cp("act", S_bf[(m + 1) % 2], S32, [("S32", 0), ("S32", 1)], [("S_bf", (m + 1) % 2)])
        act(osq, bank_f[bO], AF.Square, [nO], ["osq"])
        P.add("dve", lambda e: e.tensor_reduce(out=st4[1], in_=osq.rearrange("p (a b) -> p a b", a=4),
                                               axis=AX.X, op=ALU.add), ["osq"], ["st1"])
        rstd_from_ss(st4[1], st4[1], 1.0 / 128, ["st1", "eps_t"], ["st1"])
        mc = m % 3
        for h in range(4):
            stt(mixcat[mc][:, h * 128:(h + 1) * 128], bank_f[bO][:, h * 128:(h + 1) * 128],
                st4[1][:, h:h + 1], t2[sl][:, h * 128:(h + 1) * 128], ALU.mult, ALU.mult,
                [nO, "st1", ("t2", sl)], [("mixcat", mc, h)])

    def swa_pv(t):
        mc = t % 3
        js = [j for j in (-1, 0, 1) if 0 <= t + j < NT]
        for kv in range(2):
            bV, nV = pbank("G")
            for g in range(4):
                for j in js:
                    k5 = (t + j) % 5
                    mm(bank_f[bV][:, g * 65:(g + 1) * 65], pT[t % 2][kv][j + 1][:, g * 128:(g + 1) * 128],
                       vsa[k5][:, kv, :], j == js[0], j == js[-1],
                       [("pT", t % 2, kv, j + 1), ("vsa", k5)], [nV])
            o3 = bank_f[bV][:, 0:260].rearrange("p (a b) -> p a b", a=4)
            dk = den[:, kv * 4:(kv + 1) * 4]
            tt("dve", dk.rearrange("p (a b) -> p a b", b=1), o3[:, :, 64:65],
               esink[:, kv * 4:(kv + 1) * 4].rearrange("p (a b) -> p a b", b=1), ALU.add,
               [nV, "esink"], [("den", kv)])
            P.add("dve", lambda e, dk=dk: e.reciprocal(out=dk, in_=dk), [("den", kv)], [("den", kv)])
            c0 = 512 + kv * 256
            tt("dve", mixcat[mc][:, c0:c0 + 256].rearrange("p (a b) -> p a b", a=4), o3[:, :, 0:64],
               dk.unsqueeze(2).to_broadcast([128, 4, 64]), ALU.mult,
               [nV, ("den", kv)], [("mixcat", mc, 4 + kv)])

    def mix_tr(t):
        mc = t % 3
        bT, nT_ = pbank("T")
        for kc in range(8):
            P.add("pe", lambda e, kc=kc, bT=bT: e.transpose(
                bank_h[bT][:, kc * 128:(kc + 1) * 128], mixcat[mc][:, kc * 128:(kc + 1) * 128], ident_b),
                [("mixcat", mc, q_) for q_ in range(6)] + ["ident_b"], [nT_])
        cp("act", mixT, bank_h[bT].rearrange("p (a b) -> p a b", a=8), [nT_], ["mixT"])

    def outproj(t):
        xsl = t % 2
        dma("sp", xr[xsl], x_d[t * 128:(t + 1) * 128, :], (), [("xr", xsl)])
        hb = []
        for half in range(2):
            bi, bn = pbank("P")
            for kc in range(8):
                mm(bank_f[bi], mixT[:, kc, :], W_out[:, kc, half * 512:(half + 1) * 512],
                   kc == 0, kc == 7, ["mixT", "W_out"], [bn])
            act(junk[:, 0:512], bank_f[bi], AF.Square, [bn], ["junk", ("st2", half)],
                accum_out=st4[2][:, half:half + 1])
            hb.append((bi, bn))
        tt("dve", st4[2][:, 2:3], st4[2][:, 0:1], st4[2][:, 1:2], ALU.add,
           [("st2", 0), ("st2", 1)], [("st2", 2)])
        rstd_from_ss(st4[2][:, 2:3], st4[2][:, 3:4], 1.0 / D, [("st2", 2), "eps_t"], [("st2", 3)])
        for half in range(2):
            bi, bn = hb[half]
            stt(hs[xsl][:, half * 512:(half + 1) * 512], bank_f[bi], st4[2][:, 3:4],
                gpost[:, half * 512:(half + 1) * 512], ALU.mult, ALU.mult,
                [bn, ("st2", 3), "gpost"], [("hs", xsl, half)])
        tt("dve", hs[xsl], hs[xsl], xr[xsl], ALU.add, [("hs", xsl, 0), ("hs", xsl, 1), ("xr", xsl)],
           [("hs", xsl, 0), ("hs", xsl, 1)])
        dma("sp", hbuf_d[t * 128:(t + 1) * 128, :], hs[xsl], [("hs", xsl, 0), ("hs", xsl, 1)], [("hbuf", t)])

    if upto >= 2:
        for n in range(NT + 3):
            m, tl, tp = n - 1, n - 2, n - 3
            okn, okm = n < NT, 0 <= m < NT
            okl, okp = 0 <= tl < NT, 0 <= tp < NT
            s2 = upto >= 2.2
            s3 = upto >= 2.3
            if n + 2 < NT:
                load_x(n + 2)
            lg = okl and s3
            if okn:
                projA(n)
            if okp and s3:
                swa_pv(tp)
            if lg:
                swa_logits(tl, 0, -1)
            if okn:
                projB(n)
                if n + 1 < NT:
                    normT_stats(n + 1)
            if lg:
                swa_logits(tl, 0, 0)
            if okm and s2:
                S2a(m)
            if okn:
                projC(n)
            if lg:
                swa_logits(tl, 0, 1)
            if okn:
                projC2(n)
            if okp and s3:
                mix_tr(tp)
            if lg:
                swa_logits(tl, 1, -1)
            if okn:
                projC3(n)
                projC2_tr(n)
                if n + 1 < NT:
                    normT_tr(n + 1)
            if lg:
                swa_logits(tl, 1, 0)
            if okm and s2:
                S2b(m)
            if okn:
                projC3_tr(n)
            if lg:
                swa_logits(tl, 1, 1)
            if okp and s3:
                outproj(tp)
            if okn:
                projG(n)
            if okm and s2:
                S2c(m)

    P.barrier()
    state["peak_main"] = state["peak"]
    state["off"] = mark_persist
    W_up = alloc([8, DFF], BF16)
    W_dn = alloc([32, D], BF16)
    G = 2
    NG = (NT + G - 1) // G
    aT = [alloc([32, G * 128], BF16) for _ in range(2)]
    hnT = [alloc([8, G * 128], BF16) for _ in range(2)]
    hsb = [alloc([D]) for _ in range(4)]
    junk2 = alloc([D], BF16)
    u2 = alloc([D], BF16)
    ot = [alloc([D]) for _ in range(2)]
    stf = [alloc([4]) for _ in range(3)]
    relu_s = [alloc([G * 128]) for _ in range(2)]

    w_up_v = w_up_d.rearrange("(kc p) c -> p kc c", p=128)
    for q in range(4):
        for kc in range(8):
            dma("pool", W_up[:, kc, q * 1024:(q + 1) * 1024], w_up_v[:, kc, q * 1024:(q + 1) * 1024],
                (), [("W_up", kc, q)])
    w_dn_v = w_down_d.rearrange("(j p) c -> p j c", p=128)
    for j4 in range(8):
        dma("pool", W_dn[:, j4 * 4:(j4 + 1) * 4, :], w_dn_v[:, j4 * 4:(j4 + 1) * 4, :], (), [("W_dn", j4)])

    def prep_stats(gi, tt_):
        t = gi * G + tt_
        hsl = t % 4
        dma("sp", hsb[hsl], hbuf_d[t * 128:(t + 1) * 128, :], [("hbuf", t)], [("hsb", hsl)])
        act(junk2, hsb[hsl], AF.Square, [("hsb", hsl)], ["junk2", "stf0"], accum_out=stf[0][:, 0:1])
        rstd_from_ss(stf[0][:, 0:1], stf[0][:, 1:2], 1.0 / D, ["stf0", "eps_t"], ["stf0"])
        stt(u2, hsb[hsl], stf[0][:, 1:2], gmpre, ALU.mult, ALU.mult,
            [("hsb", hsl), "stf0", "gmpre"], ["u2"])

    def prep_tr(gi, tt_):
        gs = gi % 2
        bi, bn = pbank("T")
        for kc in range(8):
            P.add("pe", lambda e, kc=kc, bi=bi: e.transpose(
                bank_h[bi][:, kc * 128:(kc + 1) * 128], u2[:, kc * 128:(kc + 1) * 128], ident_b),
                ["u2", "ident_b"], [bn])
        cp("act", hnT[gs][:, :, tt_ * 128:(tt_ + 1) * 128],
           bank_h[bi].rearrange("p (a b) -> p a b", a=8), [bn], [("hnT", gs, tt_)])

    def ntiles(gi):
        return min(G, NT - gi * G)

    def up(gi, j):
        gs = gi % 2
        ntok = ntiles(gi) * 128
        bi, bn = pbank("P")
        for kc in range(8):
            mm(bank_f[bi][:, 0:ntok], W_up[:, kc, j * 128:(j + 1) * 128], hnT[gs][:, kc, 0:ntok],
               kc == 0, kc == 7, [("W_up", kc, j // 8)] + [("hnT", gs, q) for q in range(G)], [bn])
        rs = j % 2
        act(relu_s[rs][:, 0:ntok], bank_f[bi][:, 0:ntok], AF.Relu, [bn], [("rr", rs)])
        tt("pool", aT[gs][:, j, 0:ntok], relu_s[rs][:, 0:ntok], relu_s[rs][:, 0:ntok], ALU.mult,
           [("rr", rs)], [("aT", gs, j)])

    def down(gi, tt_):
        gs = gi % 2
        t = gi * G + tt_
        osl = t % 2
        hsl = t % 4
        hb = []
        for half in range(2):
            bi, bn = pbank("G")
            for j in range(32):
                mm(bank_f[bi], aT[gs][:, j, tt_ * 128:(tt_ + 1) * 128],
                   W_dn[:, j, half * 512:(half + 1) * 512], j == 0, j == 31,
                   [("aT", gs, j), ("W_dn", j // 4)], [bn])
            act(junk2[:, 0:512], bank_f[bi], AF.Square, [bn], ["junk2", ("stf1", half)],
                accum_out=stf[1][:, half:half + 1])
            hb.append((bi, bn))
        tt("dve", stf[1][:, 2:3], stf[1][:, 0:1], stf[1][:, 1:2], ALU.add,
           [("stf1", 0), ("stf1", 1)], [("stf1", 2)])
        rstd_from_ss(stf[1][:, 2:3], stf[1][:, 3:4], 1.0 / D, [("stf1", 2), "eps_t"], [("stf1", 3)])
        for half in range(2):
            bi, bn = hb[half]
            stt(ot[osl][:, half * 512:(half + 1) * 512], bank_f[bi], stf[1][:, 3:4],
                gmpost[:, half * 512:(half + 1) * 512], ALU.mult, ALU.mult,
                [bn, ("stf1", 3), "gmpost"], [("ot", osl, half)])
        tt("dve", ot[osl], ot[osl], hsb[hsl], ALU.add, [("ot", osl, 0), ("ot", osl, 1), ("hsb", hsl)],
           [("ot", osl, 0), ("ot", osl, 1)])
        dma("sp", out_d[t * 128:(t + 1) * 128, :], ot[osl], [("ot", osl, 0), ("ot", osl, 1)], [("out", t)])

    if upto >= 3:
        def up_group(gi, with_prep):
            nxt = gi + 1 if gi + 1 < NG else None
            for j in range(32):
                up(gi, j)
                if nxt is not None and with_prep:
                    for tt_ in range(ntiles(nxt)):
                        if j == 4 + 12 * tt_:
                            prep_stats(nxt, tt_)
                        if j == 12 + 12 * tt_:
                            prep_tr(nxt, tt_)

        for tt_ in range(ntiles(0)):
            prep_stats(0, tt_)
            prep_tr(0, tt_)
        up_group(0, True)
        if NG > 1:
            up_group(1, False)
        for tt_ in range(ntiles(0)):
            down(0, tt_)
        if NG > 2:
            for tt_ in range(ntiles(2)):
                prep_stats(2, tt_)
                prep_tr(2, tt_)
        if NG > 1:
            for tt_ in range(ntiles(1)):
                down(1, tt_)
        for gi in range(2, NG):
            up_group(gi, True)
            for tt_ in range(ntiles(gi)):
                down(gi, tt_)

    P.add("sp", None, ["out"] + (["hbuf"] if debug else []), ())

    P.emit(nc, es)
    es.close()
    return nc, P, state


def _t5_buckets(rel):
    nb = 16
    ret = (rel > 0).astype(np.int32) * nb
    n = np.abs(rel)
    max_exact = nb // 2
    large = max_exact + (np.log(np.maximum(n, 1).astype(np.float32) / max_exact)
                         / math.log(128 / max_exact) * (nb - max_exact)).astype(np.int32)
    large = np.minimum(large, nb - 1)
    return ret + np.where(n < max_exact, n, large)


def _constants():
    s = np.arange(128)[:, None]
    c = np.arange(128)[None, :]
    v = np.float32(-1.0 / 16.0)
    tri = np.stack([(s <= c), (s >= c), (s < c), (s > c)]).astype(np.float32) * v
    mask = np.stack([(s <= c), (s >= c)]).astype(np.float32)
    ident = np.eye(128, dtype=np.float32)
    flip = np.ascontiguousarray(ident[::-1])
    oh = np.zeros((33, 512), np.float32)
    rel = np.arange(512) - 256
    bk = _t5_buckets(rel)
    for i in range(512):
        if abs(int(rel[i])) <= 128:
            oh[bk[i], i] = 1.0
        else:
            oh[32, i] = 1.0
    return {"c_tri": tri, "c_mask": mask, "c_ident": ident, "c_flip": flip, "c_oh": oh}


_CACHE = {}


def _get_program(NT, debug=False):
    key = (NT, debug)
    if key not in _CACHE:
        _CACHE[key] = build_program(NT, debug=debug)
    return _CACHE[key]


def kernel(x, norm_mix_pre, w_in, w_gate_up_fwd, b_gate_fwd, w_gate_up_bwd, b_gate_bwd,
           gla_norm, swa_sink, rel_bias, w_out, norm_mix_post, norm_mlp_pre, w_up, w_down,
           norm_mlp_post):
    x = np.asarray(x, dtype=np.float32)
    B, L, _ = x.shape
    NT = L // 128
    nc, _, _ = _get_program(NT)
    f = lambda a: np.ascontiguousarray(np.asarray(a, dtype=np.float32))
    shared = {
        "norm_mix_pre": f(norm_mix_pre)[0:1], "w_in": f(w_in)[0],
        "w_gate_up_fwd": f(w_gate_up_fwd)[0], "b_gate_fwd": f(b_gate_fwd)[0:1],
        "w_gate_up_bwd": f(w_gate_up_bwd)[0], "b_gate_bwd": f(b_gate_bwd)[0:1],
        "gla_norm": f(gla_norm)[0:1], "swa_sink": f(swa_sink)[0:1], "rel_bias": f(rel_bias),
        "w_out": f(w_out)[0], "norm_mix_post": f(norm_mix_post)[0:1],
        "norm_mlp_pre": f(norm_mlp_pre)[0:1], "w_up": f(w_up)[0], "w_down": f(w_down)[0],
        "norm_mlp_post": f(norm_mlp_post)[0:1],
    }
    shared.update(_constants())
    in_maps = [dict(shared, x=np.ascontiguousarray(x[b])) for b in range(B)]
    res = run_bass_kernel_spmd(nc, in_maps, core_ids=list(range(B)))
    return np.stack([np.asarray(r["out"], dtype=np.float32) for r in res.results], axis=0)
```

```python
import math
from contextlib import ExitStack

import numpy as np
import concourse.bass as bass
import concourse.mybir as mybir
from concourse.bass_utils import run_bass_kernel_spmd
from concourse.alu_op_type import AluOpType as ALU

F32 = mybir.dt.float32
BF16 = mybir.dt.bfloat16
AF = mybir.ActivationFunctionType
AX = mybir.AxisListType

D = 1024
DFF = 4096
NCOL = 2816
EPS = 1e-6
NEG = -30000.0
ENGS = ("pe", "act", "dve", "pool", "sp")


class _Op:
    __slots__ = ("eng", "fn", "deps", "dma", "sig", "sem", "val", "prev", "idx")


def _norm(r):
    if isinstance(r, str):
        return (r, None)
    return (r[0], r[1] if len(r) == 2 else tuple(r[1:]))


class Prog:
    def __init__(self):
        self.ops = []
        self.st = {}
        self.last = {}
        self.bar_deps = set()
        self.bar_pending = set()
        self.dma_since_bar = []

    def _ents(self, name, key):
        d = self.st.setdefault(name, {})
        if key is None:
            return list(d.values())
        return [d[k] for k in (None, key) if k in d]

    def add(self, eng, fn, reads=(), writes=(), dma=False):
        op = _Op()
        op.eng, op.fn, op.dma, op.sig, op.sem, op.val = eng, fn, dma, False, None, 0
        op.idx = len(self.ops)
        deps = set()
        reads = [_norm(r) for r in reads]
        writes = [_norm(w) for w in writes]
        for name, key in reads:
            for e in self._ents(name, key):
                if e[0] is not None:
                    deps.add(e[0])
        for name, key in writes:
            for e in self._ents(name, key):
                if e[0] is not None:
                    deps.add(e[0])
                deps.update(e[1])
        if eng in self.bar_pending:
            deps |= self.bar_deps
            self.bar_pending.discard(eng)
        deps.discard(op)
        op.deps = {d for d in deps if not (d.eng == "pe" and eng == "pe" and not d.dma)}
        for name, key in reads:
            d = self.st[name]
            ent = d.get(key)
            if ent is None:
                ent = d[key] = [None, []]
            if not dma:
                ent[1] = [r for r in ent[1] if r.dma or r.eng != eng]
            ent[1].append(op)
        for name, key in writes:
            d = self.st[name]
            if key is None:
                d.clear()
            d[key] = [op, []]
        self.ops.append(op)
        if dma:
            self.dma_since_bar.append(op)
        else:
            self.last[eng] = op
        return op

    def barrier(self):
        self.bar_deps = set(self.last.values()) | set(self.dma_since_bar)
        self.dma_since_bar = []
        self.bar_pending = set(ENGS)

    def emit(self, nc, es, ndma=8):
        handles = {"pe": nc.tensor, "act": nc.scalar, "dve": nc.vector,
                   "pool": nc.gpsimd, "sp": nc.sync}
        for op in self.ops:
            if op.dma:
                op.sig = True
            for d in op.deps:
                d.sig = True
        csem = {e: es.enter_context(nc.semaphore("c_" + e)) for e in ENGS}
        dsem = {e: [es.enter_context(nc.semaphore("d_%s%d" % (e, i))) for i in range(ndma)]
                for e in ("sp", "pool", "act")}
        ccnt = {e: 0 for e in ENGS}
        drr = {e: 0 for e in dsem}
        dcnt = {}
        for op in self.ops:
            if not op.sig:
                continue
            if op.dma:
                k = drr[op.eng] % ndma
                drr[op.eng] += 1
                dcnt[(op.eng, k)] = dcnt.get((op.eng, k), 0) + 16
                op.sem, op.val = dsem[op.eng][k], dcnt[(op.eng, k)]
                op.prev = op.val - 16
            else:
                ccnt[op.eng] += 1
                op.sem, op.val = csem[op.eng], ccnt[op.eng]
        per = {e: [o for o in self.ops if o.eng == e] for e in ENGS}
        nwaits = {e: 0 for e in ENGS}

        def run(eng, h):
            waited = {}
            for op in per[eng]:
                need = {}
                for d in op.deps:
                    k = id(d.sem)
                    if k not in need or need[k][1] < d.val:
                        need[k] = (d.sem, d.val)
                if op.dma and op.sig and op.prev > 0:
                    k = id(op.sem)
                    if k not in need or need[k][1] < op.prev:
                        need[k] = (op.sem, op.prev)
                for k, (sem, val) in need.items():
                    if waited.get(k, 0) < val:
                        h.wait_ge(sem, val)
                        waited[k] = val
                        nwaits[eng] += 1
                if op.fn is not None:
                    ins = op.fn(h)
                    if op.sig:
                        ins.then_inc(op.sem, 16 if op.dma else 1)

        block = es.enter_context(nc.Block())

        @block.tensor
        def _(e):
            run("pe", e)

        @block.scalar
        def _(e):
            run("act", e)

        @block.vector
        def _(e):
            run("dve", e)

        @block.gpsimd
        def _(e):
            run("pool", e)

        @block.sync
        def _(e):
            run("sp", e)

        self.stats = {e: (len(per[e]), nwaits[e]) for e in ENGS}


def build_program(NT, debug=False, arena_kib=207, upto=3):
    L = NT * 128
    nc = bass.Bass("TRN2", target_bir_lowering=False)
    es = ExitStack()
    P = Prog()

    def din(name, shape):
        return nc.dram_tensor(name, list(shape), F32, kind="ExternalInput").ap()

    x_d = din("x", [L, D])
    g_pre_d = din("norm_mix_pre", [1, D])
    w_in_d = din("w_in", [D, 2336])
    wgf_d = din("w_gate_up_fwd", [16, 256])
    bgf_d = din("b_gate_fwd", [1, 256])
    wgb_d = din("w_gate_up_bwd", [16, 256])
    bgb_d = din("b_gate_bwd", [1, 256])
    gn_d = din("gla_norm", [1, 128])
    sink_d = din("swa_sink", [1, 8])
    relb_d = din("rel_bias", [32, 8])
    w_out_d = din("w_out", [D, D])
    g_post_d = din("norm_mix_post", [1, D])
    g_mpre_d = din("norm_mlp_pre", [1, D])
    w_up_d = din("w_up", [D, DFF])
    w_down_d = din("w_down", [DFF, D])
    g_mpost_d = din("norm_mlp_post", [1, D])
    c_tri_d = din("c_tri", [4, 128, 128])
    c_mask_d = din("c_mask", [2, 128, 128])
    c_ident_d = din("c_ident", [128, 128])
    c_flip_d = din("c_flip", [128, 128])
    c_oh_d = din("c_oh", [33, 512])
    out_d = nc.dram_tensor("out", [L, D], F32, kind="ExternalOutput").ap()
    hbuf_d = nc.dram_tensor("hbuf", [L, D], F32,
                            kind="ExternalOutput" if debug else "Internal").ap()
    vd_t = nc.dram_tensor("vd_scr", [8, 512], F32)
    vd_d = vd_t.ap()

    AW = arena_kib * 256
    arena = es.enter_context(nc.sbuf_tensor("arena", [128, AW], F32))
    state = {"off": 0, "peak": 0}

    def alloc(shape, dt=F32, parts=128):
        n = 1
        for s in shape:
            n *= s
        words = n if dt == F32 else (n + 1) // 2
        words = (words + 15) // 16 * 16
        off = state["off"]
        assert off + words <= AW, ("SBUF arena overflow", off, words, AW)
        state["off"] = off + words
        state["peak"] = max(state["peak"], state["off"])
        v = arena[0:parts, off:off + words]
        if dt != F32:
            v = v.bitcast(dt)
        v = v[:, 0:n]
        if len(shape) == 2:
            v = v.rearrange("p (a b) -> p a b", a=shape[0])
        elif len(shape) == 3:
            v = v.rearrange("p (a b c) -> p a b c", a=shape[0], b=shape[1])
        return v

    banks = [es.enter_context(nc.psum_tensor("psb%d" % i, [128, 512], F32)) for i in range(8)]
    bank_f = [b[:, :] for b in banks]
    bank_h = [b[:, :].bitcast(BF16) for b in banks]
    classes = {"T": [0, 1], "P": [2, 3, 4], "G": [5, 6, 7]}
    rr = {k: 0 for k in classes}

    touch = {}

    def pbank(cls):
        best, bt = None, None
        for i in range(8):
            nm = "ps%d" % i
            tch = touch.get(nm, -1)
            ents = P.st.get(nm, {})
            for e in ents.values():
                if e[0] is not None:
                    tch = max(tch, e[0].idx)
                for r in e[1]:
                    tch = max(tch, r.idx)
            if bt is None or tch < bt:
                best, bt = i, tch
        touch["ps%d" % best] = len(P.ops)
        return best, "ps%d" % best

    def dma(eng, out, in_, reads, writes):
        return P.add(eng, lambda e: e.dma_start(out=out, in_=in_), reads, writes, dma=True)

    def act(out, in_, func, reads, writes, **kw):
        return P.add("act", lambda e: e.activation(out=out, in_=in_, func=func, **kw), reads, writes)

    def mm(out, lhsT, rhs, start, stop, reads, writes):
        return P.add("pe", lambda e: e.matmul(out, lhsT, rhs, start=start, stop=stop), reads, writes)

    def tt(eng, out, in0, in1, op, reads, writes):
        return P.add(eng, lambda e: e.tensor_tensor(out=out, in0=in0, in1=in1, op=op), reads, writes)

    def stt(out, in0, scalar, in1, op0, op1, reads, writes):
        return P.add("dve", lambda e: e.scalar_tensor_tensor(out=out, in0=in0, scalar=scalar, in1=in1,
                                                             op0=op0, op1=op1), reads, writes)

    def ts(eng, out, in0, s1, s2, op0, op1, reads, writes):
        return P.add(eng, lambda e: e.tensor_scalar(out=out, in0=in0, scalar1=s1, scalar2=s2,
                                                    op0=op0, op1=op1), reads, writes)

    def cp(eng, out, in_, reads, writes):
        if eng == "act":
            return act(out, in_, AF.Copy, reads, writes)
        return P.add(eng, lambda e: e.tensor_copy(out=out, in_=in_), reads, writes)

    def mset(eng, out, val, writes):
        return P.add(eng, lambda e: e.memset(out, val), (), writes)

    def bcast(ap_row):
        return ap_row.partition_broadcast(128)

    ident_b = alloc([128], BF16)
    gmpre = alloc([D]); gmpost = alloc([D])
    eps_t = alloc([1])
    mark_persist = state["off"]
    ident_f = alloc([128]); flipJ = alloc([128]); tri = alloc([4, 128])
    maskF4 = alloc([512], BF16); maskB4 = alloc([512], BF16)
    gpre = alloc([D]); gpost = alloc([D])
    gncol = alloc([1])
    esink = alloc([8])
    ones2 = alloc([128], BF16, parts=33)
    Rb = alloc([512], BF16, parts=33)

    dma("sp", ident_f, c_ident_d, (), ["ident_f"])
    dma("sp", flipJ, c_flip_d, (), ["flipJ"])
    dma("sp", tri, c_tri_d.rearrange("k p c -> p k c"), (), ["tri"])
    dma("sp", gpre, bcast(g_pre_d), (), ["gpre"])
    dma("sp", gpost, bcast(g_post_d), (), ["gpost"])
    dma("sp", gmpre, bcast(g_mpre_d), (), ["gmpre"])
    dma("sp", gmpost, bcast(g_mpost_d), (), ["gmpost"])
    dma("sp", gncol, gn_d.rearrange("o e -> e o"), (), ["gncol"])
    dma("sp", esink, bcast(sink_d), (), ["esink"])
    act(esink, esink, AF.Exp, ["esink"], ["esink"])
    cp("dve", ident_b, ident_f, ["ident_f"], ["ident_b"])
    mset("dve", eps_t, EPS, ["eps_t"])

    W_all = alloc([8, NCOL], BF16)
    W_out = alloc([8, D], BF16)
    Sb_all = alloc([NT, 2, 128], BF16)
    biasT = alloc([3, 8, 128])
    mark_stream = state["off"]

    mtmp = alloc([2, 128])
    wz32 = alloc([8, 32])
    wzT = alloc([1024], parts=32)
    BD = alloc([512], parts=32)
    b32 = alloc([512], parts=33)
    hi_b = alloc([512], BF16, parts=33)
    lo_f = alloc([512], parts=33)
    tabext = alloc([8], parts=33)
    OH = alloc([512], parts=33)
    Vsb = alloc([512], parts=8)
    Hk = alloc([3, 8, 128])
    wo32 = alloc([4, D])

    dma("sp", mtmp, c_mask_d.rearrange("k p c -> p k c"), (), ["mtmp"])
    for h in range(4):
        cp("dve", maskF4[:, h * 128:(h + 1) * 128], mtmp[:, 0, :], ["mtmp"], [("maskF4", h)])
        cp("dve", maskB4[:, h * 128:(h + 1) * 128], mtmp[:, 1, :], ["mtmp"], [("maskB4", h)])

    w_in_v = w_in_d.rearrange("(kc p) c -> p kc c", p=128)
    dma("pool", W_all[:, :, 0:1536], w_in_v[:, :, 0:1536], (), [("W_all", "main")])
    for kc in range(8):
        for kv in range(2):
            src = w_in_d[kc * 128:(kc + 1) * 128, 1568 + kv * 256:1568 + (kv + 1) * 256].rearrange(
                "p (g d) -> p g d", g=4)
            dst = W_all[:, kc, 2048:2560].rearrange("p (g kv d) -> p kv g d", g=4, kv=2)[:, kv, :, :]
            dma("pool", dst, src, (), [("W_all", "qs%d_%d" % (kc, kv))])
    dma("pool", W_all[:, :, 2560:2816], w_in_v[:, :, 2080:2336], (), [("W_all", "kv")])
    w_out_v = w_out_d.rearrange("(kc p) c -> p kc c", p=128)
    dma("pool", W_out[:, 4:8, :], w_out_v[:, 4:8, :], (), [("W_out", "swa")])
    dma("sp", wo32, w_out_v[:, 0:4, :], (), ["wo32"])
    for kc in range(4):
        if kc % 2:
            act(W_out[:, kc, :], wo32[:, kc, :], AF.Copy, ["wo32", "gncol"], [("W_out", kc)],
                scale=gncol[:, 0:1])
        else:
            ts("dve", W_out[:, kc, :], wo32[:, kc, :], gncol[:, 0:1], None, ALU.mult, ALU.bypass,
               ["wo32", "gncol"], [("W_out", kc)])

    dma("sp", wz32, w_in_v[:, :, 1536:1568], (), ["wz32"])
    mset("dve", BD, 0.0, ["BD"])
    dma("sp", BD[0:16, 0:256], wgf_d, (), ["BD"])
    dma("sp", BD[16:32, 256:512], wgb_d, (), ["BD"])
    for half in range(2):
        bi, bn = pbank("P")
        for k4 in range(4):
            kc = half * 4 + k4
            P.add("pe", lambda e, kc=kc, k4=k4, bi=bi: e.transpose(
                bank_f[bi][0:32, k4 * 128:(k4 + 1) * 128], wz32[:, kc, :], ident_f),
                ["wz32", "ident_f"], [bn])
        cp("act", wzT[:, half * 512:(half + 1) * 512], bank_f[bi][0:32, :], [bn], [("wzT", half)])
    for kc in range(8):
        bi, bn = pbank("P")
        mm(bank_f[bi], wzT[:, kc * 128:(kc + 1) * 128], BD, True, True, ["wzT", "BD"], [bn])
        cp("act" if kc % 2 else "dve", W_all[:, kc, 1536:2048], bank_f[bi], [bn], [("W_all", "z%d" % kc)])

    mset("dve", b32, 0.0, ["b32"])
    for r in (0, 32):
        dma("sp", b32[r:r + 1, 0:256], bgf_d, (), ["b32"])
        dma("sp", b32[r:r + 1, 256:512], bgb_d, (), ["b32"])
    cp("dve", hi_b, b32, ["b32"], ["hi_b"])
    tt("dve", lo_f, b32, hi_b, ALU.subtract, ["b32", "hi_b"], ["lo_f"])
    cp("dve", Rb, hi_b, ["hi_b"], ["Rb"])
    cp("dve", Rb[32:33, :], lo_f[32:33, :], ["lo_f"], ["Rb"])
    mset("dve", ones2, 0.0, ["ones2"])
    mset("dve", ones2[0:1, :], 1.0, ["ones2"])
    mset("dve", ones2[32:33, :], 1.0, ["ones2"])

    mset("dve", tabext, NEG, ["tabext"])
    dma("sp", tabext[0:32, :], relb_d, (), ["tabext"])
    dma("sp", OH, c_oh_d, (), ["OH"])
    bi, bn = pbank("G")
    mm(bank_f[bi][0:8, :], tabext, OH, True, True, ["tabext", "OH"], [bn])
    cp("act", Vsb, bank_f[bi][0:8, :], [bn], ["Vsb"])
    dma("sp", vd_d, Vsb, ["Vsb"], ["vd"])
    for jj in range(3):
        j = jj - 1
        src = bass.AP(vd_d.tensor, j * 128 + 129, [[1, 128], [512, 8], [1, 128]])
        dma("sp", Hk[:, jj, :, :], src, ["vd"], [("Hk", jj)])
    for jj in range(3):
        for hh in range(2):
            bi, bn = pbank("G")
            for h4 in range(4):
                h = hh * 4 + h4
                mm(bank_f[bi][:, h4 * 128:(h4 + 1) * 128], Hk[:, jj, h, :], flipJ, True, True,
                   [("Hk", jj), "flipJ"], [bn])
            cp("act" if hh else "dve", biasT[:, jj, hh * 4:(hh + 1) * 4, :],
               bank_f[bi].rearrange("p (a b) -> p a b", a=4), [bn], [("biasT", jj, hh)])

    P.barrier()
    state["off"] = mark_stream

    NXS = 2
    xs = [alloc([D]) for _ in range(NXS)]
    junk = alloc([D], BF16)
    u_bf = alloc([D], BF16)
    uT = [alloc([8, 128], BF16) for _ in range(2)]
    st4 = [alloc([4]) for _ in range(4)]
    qk_bf = [alloc([512], BF16) for _ in range(2)]
    v_bf = [alloc([512], BF16) for _ in range(2)]
    sp_t = [alloc([512]) for _ in range(2)]
    t2 = [alloc([512]) for _ in range(2)]
    ks_bf = alloc([128], BF16)
    qs_bf = alloc([512], BF16)
    ksT = [alloc([2, 128], BF16) for _ in range(4)]
    vsa = [alloc([2, 65], BF16) for _ in range(5)]
    qsT = [alloc([4, 128], BF16) for _ in range(3)]
    eP = alloc([512]); eN = alloc([512]); ekf = alloc([256])
    kst = [alloc([256], BF16) for _ in range(2)]
    qd = [alloc([4, 128], BF16) for _ in range(2)]
    ki = [alloc([256], BF16) for _ in range(2)]
    ATf = alloc([512], BF16); ATb = alloc([512], BF16)
    S32 = alloc([2, 128]); S_bf = [alloc([2, 128], BF16) for _ in range(2)]
    osq = alloc([512])
    mixcat = [alloc([D], BF16) for _ in range(3)]
    mixT = alloc([8, 128], BF16)
    lgs = [alloc([512]) for _ in range(2)]
    pT = [[[alloc([512], BF16) for _ in range(3)] for _ in range(2)] for _ in range(2)]
    den = alloc([8])
    hs = [alloc([D]) for _ in range(2)]
    xr = [alloc([D]) for _ in range(2)]
    Sb32 = alloc([2, 128])
    dec = [alloc([4]) for _ in range(2)]

    for i in range(5):
        mset("pool", vsa[i], 1.0, [("vsa", i)])
    for i in range(4):
        mset("pool", ksT[i], 0.0, [("ksT", i, 0), ("ksT", i, 1)])
    for d in range(2):
        mset("pool", qd[d], 0.0, [("qd", d, 0), ("qd", d, 1)])
    mset("pool", S32, 0.0, [("S32", 0), ("S32", 1)])
    mset("pool", S_bf[0], 0.0, [("S_bf", 0)])
    mset("pool", Sb32, 0.0, [("Sb32", 0), ("Sb32", 1)])
    mset("pool", Sb_all[:, NT - 1, :, :], 0.0, [("Sb_all", NT - 1)])

    def load_x(n):
        slot = n % NXS
        dma("sp", xs[slot], x_d[n * 128:(n + 1) * 128, :], (), [("xs", slot)])

    def rstd_from_ss(ss_ap, out_ap, inv_n, rd, wr):
        act(out_ap, ss_ap, AF.Ln, rd, wr, scale=inv_n, bias=eps_t[:, 0:1])
        act(out_ap, out_ap, AF.Exp, wr, wr, scale=-0.5)

    def normT_stats(n):
        slot = n % NXS
        act(junk, xs[slot], AF.Square, [("xs", slot)], ["junk", "st0"], accum_out=st4[0][:, 0:1])
        rstd_from_ss(st4[0][:, 0:1], st4[0][:, 1:2], 1.0 / D, ["st0", "eps_t"], ["st0"])
        stt(u_bf, xs[slot], st4[0][:, 1:2], gpre, ALU.mult, ALU.mult, [("xs", slot), "st0", "gpre"], ["u_bf"])

    def normT_tr(n):
        us = n % 2
        bi, bn = pbank("T")
        for kc in range(8):
            P.add("pe", lambda e, kc=kc, bi=bi: e.transpose(
                bank_h[bi][:, kc * 128:(kc + 1) * 128], u_bf[:, kc * 128:(kc + 1) * 128], ident_b),
                ["u_bf", "ident_b"], [bn])
        cp("act", uT[us], bank_h[bi].rearrange("p (a b) -> p a b", a=8), [bn], [("uT", us)])

    def proj(uslot, lo, hi, bias_cols=None):
        bi, bn = pbank("P")
        n = hi - lo
        for kc in range(8):
            mm(bank_f[bi][:, 0:n], uT[uslot][:, kc, :], W_all[:, kc, lo:hi], kc == 0,
               kc == 7 and bias_cols is None, [("uT", uslot), "W_all"], [bn])
        if bias_cols is not None:
            mm(bank_f[bi][:, 0:n], ones2, Rb[:, bias_cols[0]:bias_cols[1]], False, True,
               ["ones2", "Rb"], [bn])
        return bi, bn

    def softplus_neg(dst, dst_reg, src, src_reg):
        act(dst, src, AF.Exp, [src_reg], [dst_reg], scale=-1.0)
        act(dst, dst, AF.Ln, [dst_reg], [dst_reg], bias=1.0)

    def pre_front(n):
        us = n % 2
        sl = n % 2
        bL, nL = proj(us, 1792, 2048, bias_cols=(256, 512))
        spb = sp_t[sl][:, 0:256]
        softplus_neg(spb, ("sp", sl), bank_f[bL][:, 0:256], nL)
        bK, nK = proj(us, 256, 512)
        bV, nV = proj(us, 512, 1024)
        cp("act", v_bf[sl], bank_f[bV], [nV], [("v_bf", sl)])
        return (bK, nK, spb, sl)

    def pre_mid(n, ctx):
        bK, nK, spb, sl = ctx
        bG, nG = pbank("G")
        mm(bank_f[bG][:, 0:256], tri[:, 2, :], spb, True, True, ["tri", ("sp", sl)], [nG])
        for hp in range(2):
            mm(bank_f[bG][:, 256 + 2 * hp:258 + 2 * hp], spb[:, hp * 128:(hp + 1) * 128],
               tri[:, 1, 0:2], True, True, ["tri", ("sp", sl)], [nG])
        act(ekf, bank_f[bG][:, 0:256], AF.Exp, [nG], ["ekf"])
        act(dec[sl], bank_f[bG][:, 256:260], AF.Exp, [nG], [("dec", sl)])
        tt("dve", kst[sl], bank_f[bK][:, 0:256], ekf, ALU.mult, [nK, "ekf"], [("kst", sl)])

    def pre_back(n):
        sl = n % 2
        bS, nS = pbank("G")
        for h in range(4):
            mm(bank_f[bS][(h % 2) * 64:(h % 2) * 64 + 64, (h // 2) * 128:(h // 2 + 1) * 128],
               kst[sl][:, h * 64:(h + 1) * 64], v_bf[sl][:, h * 128:(h + 1) * 128], True, True,
               [("kst", sl), ("v_bf", sl)], [nS])
        for hp in range(2):
            stt(Sb32[:, hp, :], Sb32[:, hp, :], dec[sl][:, 2 * hp:2 * hp + 1],
                bank_f[bS][:, hp * 128:(hp + 1) * 128], ALU.mult, ALU.add,
                [("Sb32", hp), ("dec", sl), nS], [("Sb32", hp)])
        cp("act", Sb_all[:, n - 1, :, :], Sb32, [("Sb32", 0), ("Sb32", 1)], [("Sb_all", n - 1)])

    if upto >= 1 and NT > 1:
        load_x(NT - 1)
        if NT - 2 >= 0:
            load_x(NT - 2)
        normT_stats(NT - 1)
        normT_tr(NT - 1)
        for n in range(NT - 1, 0, -1):
            nxt = n - 1 if n - 1 >= 1 else (0 if upto >= 2 else None)
            if nxt is not None:
                normT_stats(nxt)
                nn = nxt - 1 if nxt >= 1 else 1
                if 0 <= nn < NT and not (nxt == 0 and NT < 2):
                    load_x(nn)
            ctx = pre_front(n)
            if nxt is not None:
                normT_tr(nxt)
            if n + 1 <= NT - 1:
                pre_back(n + 1)
            pre_mid(n, ctx)
        pre_back(1)
    elif upto >= 2:
        load_x(0)
        if NT > 1:
            load_x(1)
        normT_stats(0)
        normT_tr(0)

    def swa_logits(t, kv, only_j=None):
        js = [j for j in (-1, 0, 1) if 0 <= t + j < NT and (only_j is None or j == only_j)]
        for j in js:
            bL, nL = pbank("G")
            k4 = (t + j) % 4
            mm(bank_f[bL], ksT[k4][:, kv, :], qsT[t % 3].rearrange("p a b -> p (a b)"),
               True, True, [("ksT", k4, 0), ("ksT", k4, 1), ("qsT", t % 3)], [nL])
            li = (kv * 3 + j + 1) % 2
            stt(lgs[li], bank_f[bL], 0.125,
                biasT[:, j + 1, kv * 4:(kv + 1) * 4, :].rearrange("p a b -> p (a b)"),
                ALU.mult, ALU.add, [nL, "biasT"], [("lgs", li)])
            act(pT[t % 2][kv][j + 1], lgs[li], AF.Exp, [("lgs", li)], [("pT", t % 2, kv, j + 1)])

    def projA(n):
        us = n % 2
        sl = n % 2
        bD, nD = proj(us, 1536, 2048, bias_cols=(0, 512))
        softplus_neg(sp_t[sl], ("sp", sl), bank_f[bD], nD)

    def projB(n):
        us = n % 2
        sl = n % 2
        bA, nA = proj(us, 0, 512)
        act(qk_bf[sl][:, 0:256], bank_f[bA][:, 0:256], AF.Copy, [nA], [("qk_bf", sl)], scale=0.125)
        cp("dve", qk_bf[sl][:, 256:512], bank_f[bA][:, 256:512], [nA], [("qk_bf", sl)])

    def projC(n):
        us = n % 2
        sl = n % 2
        bB, nB = proj(us, 512, 1024)
        cp("act", v_bf[sl], bank_f[bB], [nB], [("v_bf", sl)])

    def projG(n):
        us = n % 2
        sl = n % 2
        bC, nC = proj(us, 1024, 1536)
        act(t2[sl], bank_f[bC], AF.Exp, [nC], [("t2", sl)], scale=-1.0)
        act(t2[sl], t2[sl], AF.Ln, [("t2", sl)], [("t2", sl)], bias=1.0)
        act(t2[sl], t2[sl], AF.Exp, [("t2", sl)], [("t2", sl)], scale=-1.0)
        tt("dve", t2[sl], bank_f[bC], t2[sl], ALU.mult, [nC, ("t2", sl)], [("t2", sl)])

    def projC2(n):
        us = n % 2
        bF, nF = proj(us, 2560, 2816)
        cp("act", ks_bf, bank_f[bF][:, 0:128], [nF], ["ks_bf"])
        cp("dve", vsa[n % 5][:, :, 0:64], bank_f[bF][:, 128:256].rearrange("p (a b) -> p a b", a=2),
           [nF], [("vsa", n % 5)])

    def projC2_tr(n):
        k4 = n % 4
        bT, nT_ = pbank("T")
        P.add("pe", lambda e, bT=bT: e.transpose(bank_h[bT][:, 0:128], ks_bf, ident_b),
              ["ks_bf", "ident_b"], [nT_])
        for kv in range(2):
            cp("dve", ksT[k4][kv * 64:(kv + 1) * 64, kv, :], bank_h[bT][kv * 64:(kv + 1) * 64, 0:128],
               [nT_], [("ksT", k4, kv)])

    def projC3(n):
        us = n % 2
        bE, nE = proj(us, 2048, 2560)
        cp("act", qs_bf, bank_f[bE], [nE], ["qs_bf"])

    def projC3_tr(n):
        bT, nT_ = pbank("T")
        for g in range(4):
            P.add("pe", lambda e, g=g, bT=bT: e.transpose(
                bank_h[bT][:, g * 128:(g + 1) * 128], qs_bf[:, g * 128:(g + 1) * 128], ident_b),
                ["qs_bf", "ident_b"], [nT_])
        cp("dve", qsT[n % 3], bank_h[bT][:, 0:512].rearrange("p (a b) -> p a b", a=4),
           [nT_], [("qsT", n % 3)])

    def S2a(m):
        sl = m % 2
        bQ, nQ = pbank("T")
        for blk in range(4):
            P.add("pe", lambda e, blk=blk, bQ=bQ: e.transpose(
                bank_h[bQ][:, blk * 128:(blk + 1) * 128], qk_bf[sl][:, blk * 128:(blk + 1) * 128], ident_b),
                [("qk_bf", sl), "ident_b"], [nQ])
        bB, nB = pbank("G")
        for hp in range(2):
            mm(bank_f[bB][:, hp * 128:(hp + 1) * 128], sp_t[sl][:, hp * 128:(hp + 1) * 128],
               tri[:, 0, :], True, True, [("sp", sl), "tri"], [nB])
        for hp in range(2):
            mm(bank_f[bB][:, 256 + hp * 128:256 + (hp + 1) * 128],
               sp_t[sl][:, 256 + hp * 128:256 + (hp + 1) * 128], tri[:, 1, :], True, True,
               [("sp", sl), "tri"], [nB])
        bE, nE = pbank("G")
        mm(bank_f[bE][:, 0:256], tri[:, 3, :], sp_t[sl][:, 0:256], True, True, [("sp", sl), "tri"], [nE])
        act(eP, bank_f[bB], AF.Exp, [nB], ["eP"])
        act(eN, bank_f[bB], AF.Exp, [nB], ["eN"], scale=-1.0)
        act(ekf, bank_f[bE][:, 0:256], AF.Exp, [nE], ["ekf"])
        for d in range(2):
            for par in range(2):
                r0 = par * 64
                tt("dve",
                   qd[d][r0:r0 + 64, :, :].rearrange("p (hp par) c -> p hp par c", par=2)[:, :, par, :],
                   bank_h[bQ][r0:r0 + 64, 0:256].rearrange("p (hp c) -> p hp c", hp=2),
                   eP[r0:r0 + 64, d * 256:(d + 1) * 256].rearrange("p (hp c) -> p hp c", hp=2),
                   ALU.mult, [nQ, "eP"], [("qd", d, par)])
            tt("dve", ki[d], bank_h[bQ][:, 256:512], eN[:, d * 256:(d + 1) * 256], ALU.mult,
               [nQ, "eN"], [("ki", d)])
        tt("dve", kst[0], qk_bf[sl][:, 256:512], ekf, ALU.mult, [("qk_bf", sl), "ekf"], [("kst", 0)])

    def S2b(m):
        bAf, nAf = pbank("G")
        for h in range(4):
            c0 = (h // 2) * 128
            mm(bank_f[bAf][:, h * 128:(h + 1) * 128], ki[0][:, c0:c0 + 128],
               qd[0][:, h, :], True, True, [("ki", 0), ("qd", 0, 0), ("qd", 0, 1)], [nAf])
        tt("dve", ATf, bank_f[bAf], maskF4, ALU.mult, [nAf, "maskF4"], ["ATf"])
        bAb, nAb = pbank("G")
        for h in range(4):
            c0 = (h // 2) * 128
            mm(bank_f[bAb][:, h * 128:(h + 1) * 128], ki[1][:, c0:c0 + 128],
               qd[1][:, h, :], True, True, [("ki", 1), ("qd", 1, 0), ("qd", 1, 1)], [nAb])
        tt("dve", ATb, bank_f[bAb], maskB4, ALU.mult, [nAb, "maskB4"], ["ATb"])

    def S2c(m):
        sl = m % 2
        bO, nO = pbank("G")
        for h in range(4):
            hp = h // 2
            oo = bank_f[bO][:, h * 128:(h + 1) * 128]
            vv = v_bf[sl][:, h * 128:(h + 1) * 128]
            mm(oo, ATf[:, h * 128:(h + 1) * 128], vv, True, False, ["ATf", ("v_bf", sl)], [nO])
            mm(oo, qd[0][:, h, :], S_bf[sl][:, hp, :], False, False,
               [("qd", 0, 0), ("qd", 0, 1), ("S_bf", sl)], [nO])
            mm(oo, ATb[:, h * 128:(h + 1) * 128], vv, False, False, ["ATb", ("v_bf", sl)], [nO])
            mm(oo, qd[1][:, h, :], Sb_all[:, m, hp, :], False, True,
               [("qd", 1, 0), ("qd", 1, 1), ("Sb_all", m)], [nO])
        bS, nS = pbank("G")
        for h in range(4):
            mm(bank_f[bS][(h % 2) * 64:(h % 2) * 64 + 64, (h // 2) * 128:(h // 2 + 1) * 128],
               kst[0][:, h * 64:(h + 1) * 64], v_bf[sl][:, h * 128:(h + 1) * 128], True, True,
               [("kst", 0), ("v_bf", sl)], [nS])
        for hp in range(2):
            stt(S32[:, hp, :], S32[:, hp, :], eP[:, hp * 128 + 127:hp * 128 + 128],
                bank_f[bS][:, hp * 128:(hp + 1) * 128], ALU.mult, ALU.add, [("S32", hp), "eP", nS], [("S32", hp)])
        cp("act", S_bf[(m + 1) % 2], S32, [("S32", 0), ("S32", 1)], [("S_bf", (m + 1) % 2)])
        act(osq, bank_f[bO], AF.Square, [nO], ["osq"])
        P.add("dve", lambda e: e.tensor_reduce(out=st4[1], in_=osq.rearrange("p (a b) -> p a b", a=4),
                                               axis=AX.X, op=ALU.add), ["osq"], ["st1"])
        rstd_from_ss(st4[1], st4[1], 1.0 / 128, ["st1", "eps_t"], ["st1"])
        mc = m % 3
        for h in range(4):
            stt(mixcat[mc][:, h * 128:(h + 1) * 128], bank_f[bO][:, h * 128:(h + 1) * 128],
                st4[1][:, h:h + 1], t2[sl][:, h * 128:(h + 1) * 128], ALU.mult, ALU.mult,
                [nO, "st1", ("t2", sl)], [("mixcat", mc, h)])

    def swa_pv(t):
        mc = t % 3
        js = [j for j in (-1, 0, 1) if 0 <= t + j < NT]
        for kv in range(2):
            bV, nV = pbank("G")
            for g in range(4):
                for j in js:
                    k5 = (t + j) % 5
                    mm(bank_f[bV][:, g * 65:(g + 1) * 65], pT[t % 2][kv][j + 1][:, g * 128:(g + 1) * 128],
                       vsa[k5][:, kv, :], j == js[0], j == js[-1],
                       [("pT", t % 2, kv, j + 1), ("vsa", k5)], [nV])
            o3 = bank_f[bV][:, 0:260].rearrange("p (a b) -> p a b", a=4)
            dk = den[:, kv * 4:(kv + 1) * 4]
            tt("dve", dk.rearrange("p (a b) -> p a b", b=1), o3[:, :, 64:65],
               esink[:, kv * 4:(kv + 1) * 4].rearrange("p (a b) -> p a b", b=1), ALU.add,
               [nV, "esink"], [("den", kv)])
            P.add("dve", lambda e, dk=dk: e.reciprocal(out=dk, in_=dk), [("den", kv)], [("den", kv)])
            c0 = 512 + kv * 256
            tt("dve", mixcat[mc][:, c0:c0 + 256].rearrange("p (a b) -> p a b", a=4), o3[:, :, 0:64],
               dk.unsqueeze(2).to_broadcast([128, 4, 64]), ALU.mult,
               [nV, ("den", kv)], [("mixcat", mc, 4 + kv)])

    def mix_tr(t):
        mc = t % 3
        bT, nT_ = pbank("T")
        for kc in range(8):
            P.add("pe", lambda e, kc=kc, bT=bT: e.transpose(
                bank_h[bT][:, kc * 128:(kc + 1) * 128], mixcat[mc][:, kc * 128:(kc + 1) * 128], ident_b),
                [("mixcat", mc, q_) for q_ in range(6)] + ["ident_b"], [nT_])
        cp("act", mixT, bank_h[bT].rearrange("p (a b) -> p a b", a=8), [nT_], ["mixT"])

    def outproj(t):
        xsl = t % 2
        dma("sp", xr[xsl], x_d[t * 128:(t + 1) * 128, :], (), [("xr", xsl)])
        hb = []
        for half in range(2):
            bi, bn = pbank("P")
            for kc in range(8):
                mm(bank_f[bi], mixT[:, kc, :], W_out[:, kc, half * 512:(half + 1) * 512],
                   kc == 0, kc == 7, ["mixT", "W_out"], [bn])
            act(junk[:, 0:512], bank_f[bi], AF.Square, [bn], ["junk", ("st2", half)],
                accum_out=st4[2][:, half:half + 1])
            hb.append((bi, bn))
        tt("dve", st4[2][:, 2:3], st4[2][:, 0:1], st4[2][:, 1:2], ALU.add,
           [("st2", 0), ("st2", 1)], [("st2", 2)])
        rstd_from_ss(st4[2][:, 2:3], st4[2][:, 3:4], 1.0 / D, [("st2", 2), "eps_t"], [("st2", 3)])
        for half in range(2):
            bi, bn = hb[half]
            stt(hs[xsl][:, half * 512:(half + 1) * 512], bank_f[bi], st4[2][:, 3:4],
                gpost[:, half * 512:(half + 1) * 512], ALU.mult, ALU.mult,
                [bn, ("st2", 3), "gpost"], [("hs", xsl, half)])
        tt("dve", hs[xsl], hs[xsl], xr[xsl], ALU.add, [("hs", xsl, 0), ("hs", xsl, 1), ("xr", xsl)],
           [("hs", xsl, 0), ("hs", xsl, 1)])
        dma("sp", hbuf_d[t * 128:(t + 1) * 128, :], hs[xsl], [("hs", xsl, 0), ("hs", xsl, 1)], [("hbuf", t)])

    if upto >= 2:
        for n in range(NT + 3):
            m, tl, tp = n - 1, n - 2, n - 3
            okn, okm = n < NT, 0 <= m < NT
            okl, okp = 0 <= tl < NT, 0 <= tp < NT
            s2 = upto >= 2.2
            s3 = upto >= 2.3
            if n + 2 < NT:
                load_x(n + 2)
            lg = okl and s3
            if okn:
                projA(n)
            if okp and s3:
                swa_pv(tp)
            if lg:
                swa_logits(tl, 0, -1)
            if okn:
                projB(n)
                if n + 1 < NT:
                    normT_stats(n + 1)
            if lg:
                swa_logits(tl, 0, 0)
            if okm and s2:
                S2a(m)
            if okn:
                projC(n)
            if lg:
                swa_logits(tl, 0, 1)
            if okn:
                projC2(n)
            if okp and s3:
                mix_tr(tp)
            if lg:
                swa_logits(tl, 1, -1)
            if okn:
                projC3(n)
                projC2_tr(n)
                if n + 1 < NT:
                    normT_tr(n + 1)
            if lg:
                swa_logits(tl, 1, 0)
            if okm and s2:
                S2b(m)
            if okn:
                projC3_tr(n)
            if lg:
                swa_logits(tl, 1, 1)
            if okp and s3:
                outproj(tp)
            if okn:
                projG(n)
            if okm and s2:
                S2c(m)

    P.barrier()
    state["peak_main"] = state["peak"]
    state["off"] = mark_persist
    W_up = alloc([8, DFF], BF16)
    W_dn = alloc([32, D], BF16)
    G = 2
    NG = (NT + G - 1) // G
    aT = [alloc([32, G * 128], BF16) for _ in range(2)]
    hnT = [alloc([8, G * 128], BF16) for _ in range(2)]
    hsb = [alloc([D]) for _ in range(4)]
    junk2 = alloc([D], BF16)
    u2 = alloc([D], BF16)
    ot = [alloc([D]) for _ in range(2)]
    stf = [alloc([4]) for _ in range(3)]
    relu_s = [alloc([G * 128]) for _ in range(2)]

    w_up_v = w_up_d.rearrange("(kc p) c -> p kc c", p=128)
    for q in range(4):
        for kc in range(8):
            dma("pool", W_up[:, kc, q * 1024:(q + 1) * 1024], w_up_v[:, kc, q * 1024:(q + 1) * 1024],
                (), [("W_up", kc, q)])
    w_dn_v = w_down_d.rearrange("(j p) c -> p j c", p=128)
    for j4 in range(8):
        dma("pool", W_dn[:, j4 * 4:(j4 + 1) * 4, :], w_dn_v[:, j4 * 4:(j4 + 1) * 4, :], (), [("W_dn", j4)])

    def prep_stats(gi, tt_):
        t = gi * G + tt_
        hsl = t % 4
        dma("sp", hsb[hsl], hbuf_d[t * 128:(t + 1) * 128, :], [("hbuf", t)], [("hsb", hsl)])
        act(junk2, hsb[hsl], AF.Square, [("hsb", hsl)], ["junk2", "stf0"], accum_out=stf[0][:, 0:1])
        rstd_from_ss(stf[0][:, 0:1], stf[0][:, 1:2], 1.0 / D, ["stf0", "eps_t"], ["stf0"])
        stt(u2, hsb[hsl], stf[0][:, 1:2], gmpre, ALU.mult, ALU.mult,
            [("hsb", hsl), "stf0", "gmpre"], ["u2"])

    def prep_tr(gi, tt_):
        gs = gi % 2
        bi, bn = pbank("T")
        for kc in range(8):
            P.add("pe", lambda e, kc=kc, bi=bi: e.transpose(
                bank_h[bi][:, kc * 128:(kc + 1) * 128], u2[:, kc * 128:(kc + 1) * 128], ident_b),
                ["u2", "ident_b"], [bn])
        cp("act", hnT[gs][:, :, tt_ * 128:(tt_ + 1) * 128],
           bank_h[bi].rearrange("p (a b) -> p a b", a=8), [bn], [("hnT", gs, tt_)])

    def ntiles(gi):
        return min(G, NT - gi * G)

    def up(gi, j):
        gs = gi % 2
        ntok = ntiles(gi) * 128
        bi, bn = pbank("P")
        for kc in range(8):
            mm(bank_f[bi][:, 0:ntok], W_up[:, kc, j * 128:(j + 1) * 128], hnT[gs][:, kc, 0:ntok],
               kc == 0, kc == 7, [("W_up", kc, j // 8)] + [("hnT", gs, q) for q in range(G)], [bn])
        rs = j % 2
        act(relu_s[rs][:, 0:ntok], bank_f[bi][:, 0:ntok], AF.Relu, [bn], [("rr", rs)])
        tt("dve", aT[gs][:, j, 0:ntok], relu_s[rs][:, 0:ntok], relu_s[rs][:, 0:ntok], ALU.mult,
           [("rr", rs)], [("aT", gs, j)])

    def down(gi, tt_):
        gs = gi % 2
        t = gi * G + tt_
        osl = t % 2
        hsl = t % 4
        hb = []
        for half in range(2):
            bi, bn = pbank("G")
            for j in range(32):
                mm(bank_f[bi], aT[gs][:, j, tt_ * 128:(tt_ + 1) * 128],
                   W_dn[:, j, half * 512:(half + 1) * 512], j == 0, j == 31,
                   [("aT", gs, j), ("W_dn", j // 4)], [bn])
            act(junk2[:, 0:512], bank_f[bi], AF.Square, [bn], ["junk2", ("stf1", half)],
                accum_out=stf[1][:, half:half + 1])
            hb.append((bi, bn))
        tt("dve", stf[1][:, 2:3], stf[1][:, 0:1], stf[1][:, 1:2], ALU.add,
           [("stf1", 0), ("stf1", 1)], [("stf1", 2)])
        rstd_from_ss(stf[1][:, 2:3], stf[1][:, 3:4], 1.0 / D, [("stf1", 2), "eps_t"], [("stf1", 3)])
        for half in range(2):
            bi, bn = hb[half]
            stt(ot[osl][:, half * 512:(half + 1) * 512], bank_f[bi], stf[1][:, 3:4],
                gmpost[:, half * 512:(half + 1) * 512], ALU.mult, ALU.mult,
                [bn, ("stf1", 3), "gmpost"], [("ot", osl, half)])
        tt("dve", ot[osl], ot[osl], hsb[hsl], ALU.add, [("ot", osl, 0), ("ot", osl, 1), ("hsb", hsl)],
           [("ot", osl, 0), ("ot", osl, 1)])
        dma("sp", out_d[t * 128:(t + 1) * 128, :], ot[osl], [("ot", osl, 0), ("ot", osl, 1)], [("out", t)])

    if upto >= 3:
        def up_group(gi, with_prep):
            nxt = gi + 1 if gi + 1 < NG else None
            for j in range(32):
                up(gi, j)
                if nxt is not None and with_prep:
                    for tt_ in range(ntiles(nxt)):
                        if j == 4 + 12 * tt_:
                            prep_stats(nxt, tt_)
                        if j == 12 + 12 * tt_:
                            prep_tr(nxt, tt_)

        for tt_ in range(ntiles(0)):
            prep_stats(0, tt_)
            prep_tr(0, tt_)
        up_group(0, True)
        if NG > 1:
            up_group(1, False)
        for tt_ in range(ntiles(0)):
            down(0, tt_)
        if NG > 2:
            for tt_ in range(ntiles(2)):
                prep_stats(2, tt_)
                prep_tr(2, tt_)
        if NG > 1:
            for tt_ in range(ntiles(1)):
                down(1, tt_)
        for gi in range(2, NG):
            up_group(gi, True)
            for tt_ in range(ntiles(gi)):
                down(gi, tt_)

    P.add("sp", None, ["out"] + (["hbuf"] if debug else []), ())

    P.emit(nc, es)
    es.close()
    return nc, P, state


def _t5_buckets(rel):
    nb = 16
    ret = (rel > 0).astype(np.int32) * nb
    n = np.abs(rel)
    max_exact = nb // 2
    large = max_exact + (np.log(np.maximum(n, 1).astype(np.float32) / max_exact)
                         / math.log(128 / max_exact) * (nb - max_exact)).astype(np.int32)
    large = np.minimum(large, nb - 1)
    return ret + np.where(n < max_exact, n, large)


def _constants():
    s = np.arange(128)[:, None]
    c = np.arange(128)[None, :]
    v = np.float32(-1.0 / 16.0)
    tri = np.stack([(s <= c), (s >= c), (s < c), (s > c)]).astype(np.float32) * v
    mask = np.stack([(s <= c), (s >= c)]).astype(np.float32)
    ident = np.eye(128, dtype=np.float32)
    flip = np.ascontiguousarray(ident[::-1])
    oh = np.zeros((33, 512), np.float32)
    rel = np.arange(512) - 256
    bk = _t5_buckets(rel)
    for i in range(512):
        if abs(int(rel[i])) <= 128:
            oh[bk[i], i] = 1.0
        else:
            oh[32, i] = 1.0
    return {"c_tri": tri, "c_mask": mask, "c_ident": ident, "c_flip": flip, "c_oh": oh}


_CACHE = {}


def _get_program(NT, debug=False):
    key = (NT, debug)
    if key not in _CACHE:
        _CACHE[key] = build_program(NT, debug=debug)
    return _CACHE[key]


def kernel(x, norm_mix_pre, w_in, w_gate_up_fwd, b_gate_fwd, w_gate_up_bwd, b_gate_bwd,
           gla_norm, swa_sink, rel_bias, w_out, norm_mix_post, norm_mlp_pre, w_up, w_down,
           norm_mlp_post):
    x = np.asarray(x, dtype=np.float32)
    B, L, _ = x.shape
    NT = L // 128
    nc, _, _ = _get_program(NT)
    f = lambda a: np.ascontiguousarray(np.asarray(a, dtype=np.float32))
    shared = {
        "norm_mix_pre": f(norm_mix_pre)[0:1], "w_in": f(w_in)[0],
        "w_gate_up_fwd": f(w_gate_up_fwd)[0], "b_gate_fwd": f(b_gate_fwd)[0:1],
        "w_gate_up_bwd": f(w_gate_up_bwd)[0], "b_gate_bwd": f(b_gate_bwd)[0:1],
        "gla_norm": f(gla_norm)[0:1], "swa_sink": f(swa_sink)[0:1], "rel_bias": f(rel_bias),
        "w_out": f(w_out)[0], "norm_mix_post": f(norm_mix_post)[0:1],
        "norm_mlp_pre": f(norm_mlp_pre)[0:1], "w_up": f(w_up)[0], "w_down": f(w_down)[0],
        "norm_mlp_post": f(norm_mlp_post)[0:1],
    }
    shared.update(_constants())
    in_maps = [dict(shared, x=np.ascontiguousarray(x[b])) for b in range(B)]
    res = run_bass_kernel_spmd(nc, in_maps, core_ids=list(range(B)))
    return np.stack([np.asarray(r["out"], dtype=np.float32) for r in res.results], axis=0)
```
